# Optimizing a Trainium2 kernel written in Bass

```python
import jax, jax.numpy as jnp
from jax import lax
import numpy as np

D_MODEL = 1024
BATCH = 8
SEQ = 2048
DEPTH = 2

D_MIX = D_MODEL
ATTN_DIM = D_MIX // 2
CONV_DIM = D_MIX - ATTN_DIM
HEAD_DIM = 64
N_HEADS = ATTN_DIM // HEAD_DIM
CONV_WIDTH = 31
Q_BLOCK = 128
PLE_DIM = 256
D_IN = 4 * ATTN_DIM + 3 * CONV_DIM
EPS = 1e-6

kernel_name = "hymba_conformer_stickbreaking_ple"


def rms_norm(x, g):
    xf = x.astype(jnp.float32)
    y = xf * lax.rsqrt(jnp.mean(xf * xf, axis=-1, keepdims=True) + EPS)
    return (y * g.astype(jnp.float32)).astype(x.dtype)


def layer_norm(x, g, b):
    xf = x.astype(jnp.float32)
    mu = jnp.mean(xf, axis=-1, keepdims=True)
    xc = xf - mu
    y = xc * lax.rsqrt(jnp.mean(xc * xc, axis=-1, keepdims=True) + EPS)
    return (y * g.astype(jnp.float32) + b.astype(jnp.float32)).astype(x.dtype)


def stick_breaking_attention(q, k, v):
    S = q.shape[1]
    scale = HEAD_DIM ** -0.5
    outs = []
    for blk in range(S // Q_BLOCK):
        q0 = blk * Q_BLOCK
        kend = q0 + Q_BLOCK
        qb = q[:, q0:kend]
        kb = k[:, :kend]
        vb = v[:, :kend]
        z = jnp.einsum('bqhd,bkhd->bhqk', qb, kb).astype(jnp.float32) * scale
        qpos = q0 + jnp.arange(Q_BLOCK)[:, None]
        kpos = jnp.arange(kend)[None, :]
        causal = kpos < qpos
        log_1m_beta = jnp.where(causal, -jax.nn.softplus(z), 0.0)
        suffix = lax.cumsum(log_1m_beta, axis=3, reverse=True) - log_1m_beta
        log_a = jax.nn.log_sigmoid(z) + suffix
        a = jnp.where(causal, jnp.exp(log_a), 0.0)
        outs.append(jnp.einsum('bhqk,bkhd->bqhd', a.astype(v.dtype), vb))
    return jnp.concatenate(outs, axis=1)


def causal_depthwise_conv(x, w, b):
    rhs = w[:, None, :].astype(x.dtype)
    y = lax.conv_general_dilated(
        x, rhs, window_strides=(1,), padding=((CONV_WIDTH - 1, 0),),
        dimension_numbers=('NWC', 'WIO', 'NWC'), feature_group_count=x.shape[-1])
    return y + b.astype(x.dtype)


def setup_inputs(seed: int = 0) -> dict:
    key = jax.random.key(seed)
    ks = jax.random.split(key, 16)
    f32 = jnp.float32
    nrm = lambda k, shape, s: jax.random.normal(k, shape, f32) * s
    return {
        "x": nrm(ks[0], (BATCH, SEQ, D_MODEL), 1.0),
        "p": nrm(ks[1], (DEPTH, BATCH, SEQ, PLE_DIM), 1.0),
        "norm_g": 1.0 + nrm(ks[2], (DEPTH, D_MODEL), 0.02),
        "w_in": nrm(ks[3], (DEPTH, D_MODEL, D_IN), D_MODEL ** -0.5),
        "attn_out_g": 1.0 + nrm(ks[4], (DEPTH, HEAD_DIM), 0.02),
        "dw_w": nrm(ks[5], (DEPTH, CONV_WIDTH, CONV_DIM), CONV_WIDTH ** -0.5),
        "dw_b": nrm(ks[6], (DEPTH, CONV_DIM), 0.02),
        "conv_ln_g": 1.0 + nrm(ks[7], (DEPTH, CONV_DIM), 0.02),
        "conv_ln_b": nrm(ks[8], (DEPTH, CONV_DIM), 0.02),
        "w_pw": nrm(ks[9], (DEPTH, CONV_DIM, CONV_DIM), CONV_DIM ** -0.5),
        "conv_out_g": 1.0 + nrm(ks[10], (DEPTH, CONV_DIM), 0.02),
        "w_out": nrm(ks[11], (DEPTH, D_MIX, D_MODEL), D_MIX ** -0.5),
        "ple_norm_g": 1.0 + nrm(ks[12], (DEPTH, D_MODEL), 0.02),
        "w_ple_gate": nrm(ks[13], (DEPTH, D_MODEL, D_MODEL), D_MODEL ** -0.5),
        "w_ple": nrm(ks[14], (DEPTH, PLE_DIM, D_MODEL), PLE_DIM ** -0.5),
        "final_g": 1.0 + nrm(ks[15], (D_MODEL,), 0.02),
    }


def reference(x, p, norm_g, w_in, attn_out_g, dw_w, dw_b, conv_ln_g, conv_ln_b,
              w_pw, conv_out_g, w_out, ple_norm_g, w_ple_gate, w_ple, final_g):
    B, S, _ = x.shape
    split_at = np.cumsum([ATTN_DIM, ATTN_DIM, ATTN_DIM, ATTN_DIM,
                          CONV_DIM, CONV_DIM])
    h = x
    for i in range(DEPTH):
        hn = rms_norm(h, norm_g[i])
        u = hn @ w_in[i]
        q, k, v, g_attn, c_val, c_gate, g_conv = jnp.split(u, split_at, axis=-1)

        heads = lambda t: t.reshape(B, S, N_HEADS, HEAD_DIM)
        o = stick_breaking_attention(heads(q), heads(k), heads(v))
        o = rms_norm(o, attn_out_g[i]).reshape(B, S, ATTN_DIM)
        y_attn = o * jax.nn.silu(g_attn)

        c = c_val * jax.nn.sigmoid(c_gate)
        c = causal_depthwise_conv(c, dw_w[i], dw_b[i])
        c = jax.nn.silu(layer_norm(c, conv_ln_g[i], conv_ln_b[i]))
        c = c @ w_pw[i]
        y_conv = rms_norm(c, conv_out_g[i]) * jax.nn.silu(g_conv)

        y = jnp.concatenate([y_attn, y_conv], axis=-1) @ w_out[i]
        h = h + y

        gate = jax.nn.sigmoid(rms_norm(h, ple_norm_g[i]) @ w_ple_gate[i])
        h = h + (p[i].astype(h.dtype) @ w_ple[i]) * gate
    return rms_norm(h, final_g)
```

```python
import contextlib
import numpy as np
import concourse.bass as bass
import concourse.mybir as mybir
from concourse.bass_utils import run_bass_kernel_spmd

F32 = mybir.dt.float32
BF16 = mybir.dt.bfloat16
F16 = mybir.dt.float16
AF = mybir.ActivationFunctionType
ALU = mybir.AluOpType

P = 128
S = 2048
D = 1024
NL = 2
DIN = 3584
NH = 8
HD = 64
CW = 31
PLE = 256
TG = 512
NTG = S // TG
EPS = 1e-6
LC = 157
NCOL = NL * LC + 9
C_ID, C_NEG, C_NTRI, C_SELA, C_SELB, C_BD, C_NONA, C_NONB, C_END = 0, 128, 256, 384, 512, 640, 768, 896, 1024


class Buf:
    __slots__ = ("name", "w", "r")

    def __init__(self, name):
        self.name = name
        self.w = None
        self.r = {}


class Chan:
    def __init__(self, sem, step):
        self.sem = sem
        self.count = 0
        self.step = step


class Ring:
    def __init__(self, chans):
        self.chans = chans
        self.i = 0

    def next(self):
        c = self.chans[self.i % len(self.chans)]
        self.i += 1
        return c


class Eng:
    def __init__(self, raw, chan, safe):
        self.raw = raw
        self.chan = chan
        self.known = {}
        self.safe = safe


class T:
    def __init__(self, nc, stack):
        self.nc = nc
        self.stack = stack
        self.nsem = 0
        mk = self.chan
        self.pe = Eng(nc.tensor, mk(1), True)
        self.act = Eng(nc.scalar, mk(1), False)
        self.dve = Eng(nc.vector, mk(1), False)
        self.pool = Eng(nc.gpsimd, mk(1), False)
        self.sp = Eng(nc.sync, mk(1), False)

    def chan(self, step=16):
        self.nsem += 1
        sem = self.stack.enter_context(self.nc.semaphore("sm%d" % self.nsem))
        return Chan(sem, step)

    def op(self, eng, fn, reads=(), writes=(), chan=None):
        deps = {}

        def add(d):
            if d is None:
                return
            c, v = d
            if deps.get(c, 0) < v:
                deps[c] = v

        for b in reads:
            add(b.w)
        for b in writes:
            add(b.w)
            for c, v in b.r.items():
                add((c, v))
        if isinstance(chan, Ring):
            chan = chan.next()
            if chan.count > 0:
                deps[chan] = max(deps.get(chan, 0), chan.count)
        target = chan or eng.chan
        for c, v in deps.items():
            if c is eng.chan and eng.safe and chan is None:
                continue
            if eng.known.get(c, 0) >= v:
                continue
            eng.raw.wait_ge(c.sem, v)
            eng.known[c] = v
        ins = fn()
        target.count += target.step
        ins.then_inc(target.sem, target.step)
        for b in writes:
            b.w = (target, target.count)
            b.r = {}
        for b in reads:
            if b.r.get(target, 0) < target.count:
                b.r[target] = target.count
        return ins

    def wait_all(self, eng, chans):
        for c in chans:
            if c.count > 0:
                eng.raw.wait_ge(c.sem, c.count)


def build():
    nc = bass.Bass("TRN2", target_bir_lowering=False)
    dt = nc.dram_tensor
    xT = dt("xT", [D, S], F32, kind="ExternalInput").ap()
    pT = dt("pT", [NL, PLE, S], F32, kind="ExternalInput").ap()
    w_in = dt("w_in", [NL, D, DIN], F32, kind="ExternalInput").ap()
    w_pw = dt("w_pw", [NL, 512, 512], F32, kind="ExternalInput").ap()
    w_out = dt("w_out", [NL, D, D], F32, kind="ExternalInput").ap()
    w_gate = dt("w_gate", [NL, D, D], F32, kind="ExternalInput").ap()
    w_ple = dt("w_ple", [NL, PLE, D], F32, kind="ExternalInput").ap()
    cols_d = dt("cols", [P, NCOL], F32, kind="ExternalInput").ap()
    cst_d = dt("cst", [P, C_END], F32, kind="ExternalInput").ap()
    outT = dt("outT", [D, S], F32, kind="ExternalOutput").ap()
    hscr = dt("hscr", [D, S], F32, kind="Internal").ap()
    dgs = dt("dgs", [4, P, CW * P], BF16, kind="Internal").ap()

    with contextlib.ExitStack() as st:
        tr = T(nc, st)
        pe, act, dve, pool, sp = tr.pe, tr.act, tr.dve, tr.pool, tr.sp

        def sb(name, shape, dtype):
            return st.enter_context(nc.sbuf_tensor(name, shape, dtype))

        XR = sb("XR", [P, 8, S], BF16)
        SG = sb("SG", [P, 8, S], BF16)
        QT = sb("QT", [P, 4, S], BF16)
        KT = sb("KT", [P, 4, S], BF16)
        VV = sb("VV", [P, 16, 512], BF16)
        CP = sb("CP", [P, 4, S + 32], BF16)
        WT = sb("WT", [P, 18432], BF16)
        PT2 = [sb("PTt%d" % i, [P, 2, TG], BF16) for i in range(2)]
        T2 = [sb("T2_%d" % i, [P, TG], F32) for i in range(3)]
        AW = sb("AW", [P, 8192], BF16)
        aE_t = AW[:, 0:4096].bitcast(F32).rearrange("p (a b t) -> p a b t", a=2, b=2)
        aE = [[aE_t[:, i, j, :] for j in range(2)] for i in range(2)]
        aSP_t = AW[:, 4096:6144].bitcast(F16).rearrange("p (a b t) -> p a b t", a=2, b=2)
        aSP = [[aSP_t[:, i, j, :] for j in range(2)] for i in range(2)]
        aA_t = AW[:, 6144:8192].rearrange("p (a b t) -> p a b t", a=2, b=2)
        DGB = [AW[:, 0:CW * P].rearrange("p (w j) -> p w j", w=CW), AW[:, 4096:4096 + CW * P].rearrange("p (w j) -> p w j", w=CW)]
        DGBF = [AW[:, 0:CW * P], AW[:, 4096:4096 + CW * P]]
        aA = [[aA_t[:, i, j, :] for j in range(2)] for i in range(2)]
        aHT = [sb("aHT%d" % i, [P, TG], BF16) for i in range(2)]
        aRR = [sb("aRR%d" % i, [P, TG], BF16) for i in range(2)]
        aOSQ = sb("aOSQ", [P, TG], BF16)
        COLS = sb("COLS", [P, NCOL], F32)
        IDB = sb("IDB", [P, P], BF16)
        NEGB = sb("NEGB", [P, P], BF16)
        NTRI = sb("NTRI", [P, P], F16)
        SELA = sb("SELA", [P, P], BF16)
        SELB = sb("SELB", [P, P], BF16)
        BD = sb("BD", [P, P], BF16)
        NONA = sb("NONA", [P, P], F16)
        NONB = sb("NONB", [P, P], F16)
        VZ = sb("VZ", [P, 16, 2, P], BF16)
        ONESB = sb("ONESB", [P, P], BF16)
        NEGM = COLS[:, NL * LC + 8:NL * LC + 9]
        PSALL = st.enter_context(nc.psum_tensor("psall", [P, 8, TG], F32))
        PS = [PSALL[:, i, :] for i in range(8)]

        bXR = [[Buf("xr%d_%d" % (c, t)) for t in range(NTG)] for c in range(8)]
        bSG = [[Buf("sg%d_%d" % (c, t)) for t in range(NTG)] for c in range(8)]
        bQ = [[Buf("q%d_%d" % (c, t)) for t in range(NTG)] for c in range(4)]
        bK = [[Buf("k%d_%d" % (c, t)) for t in range(NTG)] for c in range(4)]
        bV = [Buf("v%d" % i) for i in range(16)]
        bCP = [[Buf("cp%d_%d" % (c, t)) for t in range(NTG)] for c in range(4)]
        bWT = [Buf("wt%d" % i) for i in range(4)]
        bPT2 = [Buf("pt0"), Buf("pt1")]
        bDGB = [[Buf("dgb") for w in range(CW)] for _ in range(2)]
        bDGS = [Buf("dgs%d" % i) for i in range(4)]
        bT2 = [Buf("t2_%d" % i) for i in range(3)]
        baE = [[Buf("e") for j in range(2)] for i in range(2)]
        baSP = [[Buf("sp") for j in range(2)] for i in range(2)]
        baA = [[Buf("a") for j in range(2)] for i in range(2)]
        baHT = [Buf("ht") for i in range(2)]
        baRR = [Buf("rr") for i in range(2)]
        bOSQ = Buf("osq")
        bCONST = Buf("const")
        bCOLS = Buf("cols")
        bPS = [Buf("ps%d" % i) for i in range(8)]
        bHS = [Buf("hscr%d" % t) for t in range(NTG)]
        bOUT = Buf("out")

        XRF = XR[:, :, :].rearrange("p c s -> p (c s)")
        QTF = QT[:, :, :].rearrange("p c s -> p (c s)")
        KTF = KT[:, :, :].rearrange("p c s -> p (c s)")
        VVF = VV[:, :, :].rearrange("p c s -> p (c s)")
        CPF = CP[:, :, :].rearrange("p c s -> p (c s)")
        QB = QTF.bitcast(F32).rearrange("p (c t) -> p c t", c=8)
        bQB = [[bQ[c // 2][(c % 2) * 2], bQ[c // 2][(c % 2) * 2 + 1]] for c in range(8)]
        KB = KTF.bitcast(F32).rearrange("p (c t) -> p c t", c=8)
        bKB = [[bK[c // 2][(c % 2) * 2], bK[c // 2][(c % 2) * 2 + 1]] for c in range(8)]
        VB = VVF.bitcast(F32).rearrange("p (c t) -> p c t", c=8)
        bVB = [[bV[2 * c], bV[2 * c + 1]] for c in range(8)]

        def cp_units(a0, a1):
            out = []
            for cc in range(4):
                lo, hi = max(a0, cc * (S + 32)), min(a1, (cc + 1) * (S + 32))
                if lo < hi:
                    for t in range(min(3, (lo - cc * (S + 32)) // 512), min(3, (hi - 1 - cc * (S + 32)) // 512) + 1):
                        if bCP[cc][t] not in out:
                            out.append(bCP[cc][t])
            return out

        HBs = [QB, VB]
        bHBs = [bQB, bVB]
        SQs = [KTF[:, 0:4096].rearrange("p (c t) -> p c t", c=8), CPF[:, 0:4096].rearrange("p (c t) -> p c t", c=8)]
        bSQs = [[[bK[c // 4][c % 4]] for c in range(8)], [cp_units(c * 512, (c + 1) * 512) for c in range(8)]]
        HN2s = [KTF[:, 4096:8192].rearrange("p (c t) -> p c t", c=8), CPF[:, 4096:8192].rearrange("p (c t) -> p c t", c=8)]
        bHN2s = [[[bK[2 + c // 4][c % 4]] for c in range(8)], [cp_units(4096 + c * 512, 4096 + (c + 1) * 512) for c in range(8)]]
        ACCs = [QB[:, 0:4, :], QB[:, 4:8, :]]
        bACCs = [bQB[0:4], bQB[4:8]]
        C2s = [KB[:, 0:4, :], KB[:, 4:8, :]]
        bC2s = [bKB[0:4], bKB[4:8]]
        ACBs = [VV[:, 0:4, :], VV[:, 4:8, :]]
        bACBs = [[[bV[i]] for i in range(0, 4)], [[bV[i]] for i in range(4, 8)]]
        SQCs = [VV[:, 8:12, :], VV[:, 12:16, :]]
        bSQCs = [[[bV[i]] for i in range(8, 12)], [[bV[i]] for i in range(12, 16)]]

        def xr_units(u0, n):
            return [bXR[u // 4][u % 4] for u in range(u0, u0 + n)]

        WPW = WT[:, 12288:14336].rearrange("p (c n) -> p c n", c=4)
        bWPW = bWT[3]
        QZ = [[XRF[:, (28 + 2 * a + x) * 512:(29 + 2 * a + x) * 512] for x in range(2)] for a in range(2)]
        bQZ = [[xr_units(28 + 2 * a + x, 1)[0] for x in range(2)] for a in range(2)]
        bVZ = Buf("vz")
        ring_sw = Ring([tr.chan() for _ in range(6)])
        ring_hw = Ring([tr.chan() for _ in range(6)])
        ch_w = ch_c = ch_p = ring_sw
        ch_h = ch_o = ring_hw

        def col(i):
            return COLS[:, i:i + 1]

        tr.op(sp, lambda: nc.sync.dma_start(out=COLS[:, :], in_=cols_d[:, :]), [], [bCOLS], chan=ch_h)
        bONES = Buf("ones")
        tr.op(dve, lambda: nc.vector.memset(ONESB[:, :], 1.0), [], [bONES])

        def load_consts():
            for dst, c0, c1 in ((IDB, C_ID, C_NEG), (NEGB, C_NEG, C_NTRI), (NTRI, C_NTRI, C_SELA), (SELA, C_SELA, C_SELB),
                                (SELB, C_SELB, C_BD), (BD, C_BD, C_NONA), (NONA, C_NONA, C_NONB), (NONB, C_NONB, C_END)):
                tr.op(pool, lambda dst=dst, c0=c0, c1=c1: nc.gpsimd.dma_start(out=dst[:, :], in_=cst_d[:, c0:c1]),
                      [], [bCONST], chan=ch_c)
            tr.op(pool, lambda: nc.gpsimd.memset(VZ[:, :, :, :], 0.0), [], [bVZ])

        gen_rr = [0]

        def gbank():
            i = gen_rr[0] % 8
            gen_rr[0] += 1
            return i

        t2_rr = [0]

        def t2():
            i = t2_rr[0] % 3
            t2_rr[0] += 1
            return i

        def norm_stats(hb_ap, hb_bufs, sq_ap, sq_bufs, nchunk=8, inv_n=1.0 / D, have_sq=False):
            for c in range(0 if have_sq else nchunk):
                tr.op(act, lambda c=c: nc.scalar.activation(out=sq_ap[:, c, :], in_=hb_ap[:, c, :], func=AF.Square),
                      hb_bufs[c], sq_bufs[c])
            b = gbank()

            def mm():
                ins = None
                for c in range(nchunk):
                    ins = nc.tensor.matmul(PS[b][:, :], lhsT=ONESB[:, :], rhs=sq_ap[:, c, :], start=(c == 0), stop=(c == nchunk - 1))
                return ins
            tr.op(pe, mm, [bONES] + [u for c in range(nchunk) for u in sq_bufs[c]], [bPS[b]])
            i_r = t2()
            tr.op(act, lambda: nc.scalar.activation(out=T2[i_r][:, :], in_=PS[b][:, :], func=AF.Ln, bias=EPS, scale=inv_n),
                  [bPS[b]], [bT2[i_r]])
            tr.op(act, lambda: nc.scalar.activation(out=T2[i_r][:, :], in_=T2[i_r][:, :], func=AF.Exp, scale=-0.5),
                  [bT2[i_r]], [bT2[i_r]])
            return i_r

        def norm_apply(hb_ap, hb_bufs, gcol0, i_r, out_fn, nchunk=8):
            for c in range(nchunk):
                oap, obufs = out_fn(c)
                tr.op(dve, lambda c=c, oap=oap: nc.vector.scalar_tensor_tensor(
                    out=oap, in0=hb_ap[:, c, :], scalar=col(gcol0 + c), in1=T2[i_r][:, :], op0=ALU.mult, op1=ALU.mult),
                    hb_bufs[c] + [bT2[i_r], bCOLS], obufs)

        def load_h(src, tg, src_bufs, hs):
            sv = src[:, tg * TG:(tg + 1) * TG].rearrange("(c p) n -> p c n", p=P)
            for c0_ in (0, 4):
                tr.op(sp, lambda c0_=c0_: nc.sync.dma_start(out=HBs[hs][:, c0_:c0_ + 4, :], in_=sv[:, c0_:c0_ + 4, :]),
                      src_bufs, [u for c in range(c0_, c0_ + 4) for u in bHBs[hs][c]], chan=ch_h)

        load_h(xT, 0, [], 0)
        load_h(xT, 1, [], 1)
        for tg in range(NTG):
            hs = tg % 2
            i_r = norm_stats(HBs[hs], bHBs[hs], SQs[hs], bSQs[hs])
            norm_apply(HBs[hs], bHBs[hs], 0, i_r, lambda c, tg=tg: (XR[:, c, tg * TG:(tg + 1) * TG], [bXR[c][tg]]))
            if tg + 2 < NTG:
                load_h(xT, tg + 2, [], hs)

        for l in range(NL):
            cb = l * LC
            hsrc = xT if l == 0 else hscr
            wslot = [0]

            def load_w(dram_ap, nk, ncols, slots, col0=0):
                n = nk * ncols
                base = slots[0] * 4096 + col0
                dst = WT[:, base:base + n].rearrange("p (c n) -> p c n", c=nk)
                src = dram_ap.rearrange("(c p) n -> p c n", p=P)
                step = max(1, 2048 // ncols)
                for k0 in range(0, nk, step):
                    k1 = min(nk, k0 + step)
                    tr.op(pool, lambda k0=k0, k1=k1: nc.gpsimd.dma_start(out=dst[:, k0:k1, :], in_=src[:, k0:k1, :]),
                          [], [bWT[s_] for s_ in slots], chan=ch_w)
                return dst

            def win_group(j):
                s_ = wslot[0] % 3
                wslot[0] += 1
                return load_w(w_in[l, :, j * 512:(j + 1) * 512], 8, 512, [s_]), bWT[s_]

            def fm_matmul(w_ap, wbuf, m, tg, nk=8, rhs_fn=None, rhs_bufs=None):
                b = gbank()
                if rhs_fn is None:
                    rhs_fn = lambda k: XR[:, k, tg * TG:(tg + 1) * TG]
                    rhs_bufs = [bXR[k][tg] for k in range(nk)]

                def mm():
                    ins = None
                    for k in range(nk):
                        ins = nc.tensor.matmul(PS[b][:, :], lhsT=w_ap[:, k, m * P:(m + 1) * P], rhs=rhs_fn(k),
                                               start=(k == 0), stop=(k == nk - 1))
                    return ins
                tr.op(pe, mm, wbuf + rhs_bufs, [bPS[b]])
                return b

            wv, bv = win_group(4)
            wg, bg = win_group(5)
            wq, bq = win_group(6)
            for c in range(4):
                tr.op(pool, lambda c=c: nc.gpsimd.memset(CP[:, c, 0:32], 0.0), [], [bCP[c][0]])
            if l == 0:
                load_consts()
            for tg in range(NTG):
                for m in range(4):
                    b1 = fm_matmul(wv, [bv], m, tg)
                    b2 = fm_matmul(wg, [bg], m, tg)
                    i = t2()
                    tr.op(act, lambda b2=b2, i=i: nc.scalar.activation(out=T2[i][:, :], in_=PS[b2][:, :], func=AF.Sigmoid),
                          [bPS[b2]], [bT2[i]])
                    tr.op(dve, lambda b1=b1, i=i, m=m, tg=tg: nc.vector.tensor_tensor(
                        out=CP[:, m, 30 + tg * TG:30 + (tg + 1) * TG], in0=PS[b1][:, :], in1=T2[i][:, :], op=ALU.mult),
                        [bPS[b1], bT2[i]], [bCP[m][tg]] + ([bCP[m][tg + 1]] if tg + 1 < NTG else []))
            for tg in range(NTG):
                for m in range(4):
                    b1 = fm_matmul(wq, [bq], m, tg)
                    tr.op(act, lambda b1=b1, m=m, tg=tg: nc.scalar.activation(
                        out=SG[:, 4 + m, tg * TG:(tg + 1) * TG], in_=PS[b1][:, :], func=AF.Silu), [bPS[b1]], [bSG[4 + m][tg]])
            tr.op(pool, lambda: nc.gpsimd.dma_start(out=WPW[:, :, :], in_=w_pw[l].rearrange("(c p) n -> p c n", p=P)),
                  [], [bWPW], chan=ch_w)
            wq0 = win_group(0)
            wq1 = win_group(1)

            dg_rr = [0]

            prepped = set()

            def conv_prep(tg, ch):
                if (tg, ch) in prepped or tg >= NTG:
                    return
                prepped.add((tg, ch))
                rb_ = (tg * 4 + ch) % 2
                if tg == 0:
                    for w in range(CW):
                        if w % 3 == 2:
                            tr.op(act, lambda w=w: nc.scalar.activation(
                                out=DGB[rb_][:, w, :], in_=IDB[:, :], func=AF.Copy, scale=col(cb + 33 + ch * CW + w)),
                                [bCONST, bCOLS], [bDGB[rb_][w]])
                        else:
                            tr.op(dve, lambda w=w: nc.vector.tensor_scalar(
                                out=DGB[rb_][:, w, :], in0=IDB[:, :], scalar1=col(cb + 33 + ch * CW + w), scalar2=None, op0=ALU.mult),
                                [bCONST, bCOLS], [bDGB[rb_][w]])
                    tr.op(sp, lambda: nc.sync.dma_start(out=dgs[ch, :, :], in_=DGBF[rb_]), bDGB[rb_], [bDGS[ch]], chan=ch_h)
                else:
                    tr.op(sp, lambda: nc.sync.dma_start(out=DGBF[rb_], in_=dgs[ch, :, :]), [bDGS[ch]], bDGB[rb_], chan=ch_h)

            def conv_taps(tg, cs_):
                ACC, bACC, ACB, bACB, SQC, bSQC = ACCs[cs_], bACCs[cs_], ACBs[cs_], bACBs[cs_], SQCs[cs_], bSQCs[cs_]
                conv_prep(tg, 0)
                for ch in range(4):
                    if ch > 0:
                        yield
                    if ch < 3:
                        conv_prep(tg, ch + 1)
                    else:
                        conv_prep(tg + 1, 0)
                    rb_ = (tg * 4 + ch) % 2
                    b = gbank()
                    rb = [bCP[ch][tg]] + ([bCP[ch][tg + 1]] if tg + 1 < NTG else [])
                    for w in range(CW):
                        if w == 15:
                            yield
                        tr.op(pe, lambda w=w: nc.tensor.matmul(
                            PS[b][:, :], lhsT=DGB[rb_][:, w, :], rhs=CP[:, ch, tg * TG + w:tg * TG + w + TG],
                            start=(w == 0), stop=(w == CW - 1)), [bDGB[rb_][w]] + rb, [bPS[b]])
                    bia = col(cb + 17 + ch)
                    tr.op(act, lambda: nc.scalar.activation(
                        out=ACC[:, ch, :], in_=PS[b][:, :], func=AF.Identity, bias=bia), [bPS[b], bCOLS], bACC[ch])
                    tr.op(pool, lambda: nc.gpsimd.tensor_copy(out=ACB[:, ch, :], in_=ACC[:, ch, :]), bACC[ch], bACB[ch])
                    tr.op(act, lambda: nc.scalar.activation(out=SQC[:, ch, :], in_=ACC[:, ch, :], func=AF.Square), bACC[ch], bSQC[ch])

            def conv_post(tg, cs_):
                ACC, bACC, ACB, bACB, SQC, bSQC = ACCs[cs_], bACCs[cs_], ACBs[cs_], bACBs[cs_], SQCs[cs_], bSQCs[cs_]
                C2, bC2 = C2s[cs_], bC2s[cs_]
                CS, bCS = ACB, bACB
                bm = gbank()

                def mm_mean():
                    ins = None
                    for ch in range(4):
                        ins = nc.tensor.matmul(PS[bm][:, :], lhsT=ONESB[:, :], rhs=ACB[:, ch, :], start=(ch == 0), stop=(ch == 3))
                    return ins
                tr.op(pe, mm_mean, [bCONST] + [u for ch in range(4) for u in bACB[ch]], [bPS[bm]])
                bq2 = gbank()

                def mm_sq():
                    ins = None
                    for ch in range(4):
                        ins = nc.tensor.matmul(PS[bq2][:, :], lhsT=ONESB[:, :], rhs=SQC[:, ch, :], start=(ch == 0), stop=(ch == 3))
                    return ins
                tr.op(pe, mm_sq, [bCONST] + [u for ch in range(4) for u in bSQC[ch]], [bPS[bq2]])
                i_mean = t2()
                tr.op(act, lambda: nc.scalar.mul(out=T2[i_mean][:, :], in_=PS[bm][:, :], mul=1.0 / 512), [bPS[bm]], [bT2[i_mean]])
                i_var = t2()
                tr.op(dve, lambda: nc.vector.tensor_tensor(out=T2[i_var][:, :], in0=T2[i_mean][:, :], in1=T2[i_mean][:, :], op=ALU.mult),
                      [bT2[i_mean]], [bT2[i_var]])
                tr.op(dve, lambda: nc.vector.scalar_tensor_tensor(
                    out=T2[i_var][:, :], in0=PS[bq2][:, :], scalar=1.0 / 512, in1=T2[i_var][:, :], op0=ALU.mult, op1=ALU.subtract),
                    [bPS[bq2], bT2[i_var]], [bT2[i_var]])
                tr.op(act, lambda: nc.scalar.activation(out=T2[i_var][:, :], in_=T2[i_var][:, :], func=AF.Ln, bias=EPS, scale=1.0),
                      [bT2[i_var]], [bT2[i_var]])
                tr.op(act, lambda: nc.scalar.activation(out=T2[i_var][:, :], in_=T2[i_var][:, :], func=AF.Exp, scale=-0.5),
                      [bT2[i_var]], [bT2[i_var]])
                for ch in range(4):
                    tr.op(pool, lambda ch=ch: nc.gpsimd.tensor_tensor(out=ACC[:, ch, :], in0=ACC[:, ch, :], in1=T2[i_mean][:, :], op=ALU.subtract),
                          bACC[ch] + [bT2[i_mean]], bACC[ch])
                    tr.op(dve, lambda ch=ch: nc.vector.tensor_tensor(out=ACC[:, ch, :], in0=ACC[:, ch, :], in1=T2[i_var][:, :], op=ALU.mult),
                          bACC[ch] + [bT2[i_var]], bACC[ch])
                    tr.op(act, lambda ch=ch: nc.scalar.activation(
                        out=CS[:, ch, :], in_=ACC[:, ch, :], func=AF.Silu, bias=col(cb + 25 + ch), scale=col(cb + 21 + ch)),
                        bACC[ch] + [bCOLS], bCS[ch])
                yield
                for m in range(4):
                    b = fm_matmul(WPW, [bWPW], m, tg, nk=4, rhs_fn=lambda k: CS[:, k, :],
                                  rhs_bufs=[u for k in range(4) for u in bCS[k]])
                    tr.op(act, lambda b=b, m=m: nc.scalar.copy(out=C2[:, m, :], in_=PS[b][:, :]), [bPS[b]], bC2[m])
                    tr.op(act, lambda b=b, m=m: nc.scalar.activation(out=SQC[:, m, :], in_=PS[b][:, :], func=AF.Square), [bPS[b]], bSQC[m])
                yield
                i_r = norm_stats(C2, bC2, SQC, bSQC, nchunk=4, inv_n=1.0 / 512, have_sq=True)
                norm_apply(C2, bC2, cb + 29, i_r, lambda m: (C2[:, m, :], bC2[m]), nchunk=4)
                for m in range(4):
                    tr.op(dve, lambda m=m: nc.vector.tensor_tensor(
                        out=SG[:, 4 + m, tg * TG:(tg + 1) * TG], in0=C2[:, m, :], in1=SG[:, 4 + m, tg * TG:(tg + 1) * TG], op=ALU.mult),
                        bC2[m] + [bSG[4 + m][tg]], [bSG[4 + m][tg]])

            def wq_chunk(m):
                wq, bq = wq0
                for tg in range(NTG):
                    b1 = fm_matmul(wq, [bq], m, tg)
                    tr.op(dve, lambda b1=b1, tg=tg: nc.vector.tensor_scalar(
                        out=QT[:, m, tg * TG:(tg + 1) * TG], in0=PS[b1][:, :], scalar1=HD ** -0.5, scalar2=None, op0=ALU.mult),
                        [bPS[b1]], [bQ[m][tg]])

            def wk_chunk(m):
                wq, bq = wq1
                for tg in range(NTG):
                    b1 = fm_matmul(wq, [bq], m, tg)
                    tr.op(dve, lambda b1=b1, tg=tg: nc.vector.tensor_copy(
                        out=KT[:, m, tg * TG:(tg + 1) * TG], in_=PS[b1][:, :]), [bPS[b1]], [bK[m][tg]])

            def last_fill():
                wq_chunk(0)
                yield
                wq_chunk(1)
                yield
                wk_chunk(0)
                yield
                wk_chunk(1)

            def interleave(ga, gb, points):
                seg = 0
                la, lb = True, True
                while la:
                    try:
                        next(ga)
                    except StopIteration:
                        la = False
                    if seg in points and lb:
                        try:
                            next(gb)
                        except StopIteration:
                            lb = False
                    seg += 1
                while lb:
                    try:
                        next(gb)
                    except StopIteration:
                        lb = False

            for _ in conv_taps(0, 0):
                pass
            for tg in range(NTG):
                if tg + 1 < NTG:
                    interleave(conv_taps(tg + 1, (tg + 1) % 2), conv_post(tg, tg % 2), (0, 3, 5))
                else:
                    interleave(last_fill(), conv_post(tg, tg % 2), (0, 1, 2))

            wq_chunk(2)
            wq_chunk(3)
            wk_chunk(2)
            wk_chunk(3)
            wq, bq = win_group(2)
            wg3 = win_group(3)
            for tb in range(16):
                b = gbank()

                def mm(b=b, tb=tb, wq=wq):
                    ins = None
                    for k in range(8):
                        ins = nc.tensor.matmul(PS[b][:, :], lhsT=XR[:, k, tb * P:(tb + 1) * P], rhs=wq[:, k, :],
                                               start=(k == 0), stop=(k == 7))
                    return ins
                tr.op(pe, mm, [bq] + [bXR[k][tb // 4] for k in range(8)], [bPS[b]])
                tr.op(act, lambda b=b, tb=tb: nc.scalar.copy(out=VV[:, tb, :], in_=PS[b][:, :]), [bPS[b]], [bV[tb]])
            wq, bq = wg3
            for m in range(4):
                for tg in range(NTG):
                    b1 = fm_matmul(wq, [bq], m, tg)
                    tr.op(act, lambda b1=b1, m=m, tg=tg: nc.scalar.activation(
                        out=SG[:, m, tg * TG:(tg + 1) * TG], in_=PS[b1][:, :], func=AF.Silu), [bPS[b1]], [bSG[m][tg]])

            WOUT = load_w(w_out[l], 8, 1024, [0, 1])
            bWOUT = [bWT[0], bWT[1]]
            WGATE = load_w(w_gate[l], 8, 1024, [2, 3])
            bWGATE = [bWT[2], bWT[3]]
            WPLE = load_w(w_ple[l], 2, 1024, [3], col0=4096)
            bWPLE = [bWT[3]]

            steps = []
            for hp in range(4):
                for qg in (3, 2, 1, 0):
                    K = 4 * qg + 4
                    for kb in range(K - 1, -1, -1):
                        steps.append(dict(hp=hp, qg=qg, kb=kb, K=K, c0=(0 if kb < 4 * qg else P * (kb - 4 * qg)),
                                          diag=(kb >= 4 * qg), first=(kb == K - 1), last=(kb == 0)))
            sidx = -1
            for i, stp in enumerate(steps):
                if stp["first"]:
                    sidx += 1
                stp["sp"] = sidx % 2
                stp["c0_prev"] = None if stp["first"] else steps[i - 1]["c0"]
                stp["bz"] = [(i % 3) * 2, (i % 3) * 2 + 1]
                stp["par"] = i % 2
            bo, bc = 6, 7
            prs = [(0, 64), (64, 128)]

            for a in range(2):
                tr.op(dve, lambda a=a: nc.vector.memset(QZ[a][0][64:128, :], 0.0), [], [bQZ[a][0]])
                tr.op(dve, lambda a=a: nc.vector.memset(QZ[a][1][0:64, :], 0.0), [], [bQZ[a][1]])

            def S0(stp):
                hp, qg, sp_ = stp["hp"], stp["qg"], stp["sp"]
                q0 = qg * TG
                for x in range(2):
                    p0, p1 = prs[x]
                    tr.op(dve, lambda x=x, p0=p0, p1=p1: nc.vector.tensor_copy(out=QZ[sp_][x][p0:p1, :], in_=QT[p0:p1, hp, q0:q0 + TG]),
                          [bQ[hp][qg]], [bQZ[sp_][x]])

            def SV(hp):
                for x in range(2):
                    h = 2 * hp + x
                    tr.op(dve, lambda x=x, h=h: nc.vector.tensor_copy(out=VZ[:, :, x, x * HD:(x + 1) * HD], in_=VV[:, :, h * HD:(h + 1) * HD]),
                          bV, [bVZ])

            def S1(stp):
                hp, qg, kb, c0, diag, sp_ = stp["hp"], stp["qg"], stp["kb"], stp["c0"], stp["diag"], stp["sp"]
                if stp["first"]:
                    S0(stp)
                q0 = qg * TG
                def g1():
                    ins = None
                    for x in range(2):
                        bzx = stp["bz"][x]
                        ins = nc.tensor.matmul(PS[bzx][:, c0:TG], lhsT=KT[:, hp, kb * P:(kb + 1) * P],
                                               rhs=QZ[sp_][x][:, c0:TG], start=True, stop=True)
                        if diag:
                            ins = nc.tensor.matmul(PS[bzx][:, c0:c0 + P], lhsT=IDB[:, :], rhs=NEGB[:, :], start=False, stop=True,
                                                   skip_group_check=True)
                    return ins
                tr.op(pe, g1, [bK[hp][kb // 4], bQZ[sp_][0], bQZ[sp_][1], bCONST], [bPS[stp["bz"][0]], bPS[stp["bz"][1]]])

            def S2(stp):
                c0, par = stp["c0"], stp["par"]
                b0 = stp["bz"][0]
                tr.op(act, lambda: nc.scalar.activation(out=aE_t[:, par, :, c0:TG], in_=PSALL[:, b0:b0 + 2, c0:TG], func=AF.Exp),
                      [bPS[b0], bPS[b0 + 1]], baE[par])
                tr.op(act, lambda: nc.scalar.activation(out=aSP_t[:, par, :, c0:TG], in_=aE_t[:, par, :, c0:TG], func=AF.Ln, bias=1.0),
                      baE[par], baSP[par])

            def S3(stp):
                kb, K, c0, c0_prev, par = stp["kb"], stp["K"], stp["c0"], stp["c0_prev"], stp["par"]
                if kb > 0:
                    def gc():
                        ins = None
                        for x in range(2):
                            ins = nc.tensor.matmul(PS[bc][:, c0:TG], lhsT=(NONA, NONB)[x][:, :], rhs=aSP[par][x][:, c0:TG],
                                                   start=(kb == K - 1 and x == 0), stop=(x == 1), skip_group_check=True)
                        return ins
                    tr.op(pe, gc, baSP[par] + [bCONST], [bPS[bc]])
                    S4d(stp)
                def g2():
                    ins = None
                    for x in range(2):
                        bzx = stp["bz"][x]
                        ins = nc.tensor.matmul(PS[bzx][:, c0:TG], lhsT=NTRI[:, :], rhs=aSP[par][x][:, c0:TG], start=False,
                                               stop=(c0_prev is None), skip_group_check=True)
                        if c0_prev is not None:
                            ins = nc.tensor.matmul(PS[bzx][:, c0_prev:TG], lhsT=(SELA, SELB)[x][:, :], rhs=aRR[1 - par][:, c0_prev:TG],
                                                   start=False, stop=True, skip_group_check=True)
                    return ins
                rd = baSP[par] + [bCONST] + ([baRR[1 - par]] if c0_prev is not None else [])
                tr.op(pe, g2, rd, [bPS[stp["bz"][0]], bPS[stp["bz"][1]]])

            def S4d(stp):
                c0, par = stp["c0"], stp["par"]
                if stp["kb"] > 0:
                    tr.op(dve, lambda: nc.vector.tensor_copy(out=aHT[par][:, c0:TG], in_=PS[bc][:, c0:TG]), [bPS[bc]], [baHT[par]])
                    tr.op(dve, lambda: nc.vector.scalar_tensor_tensor(
                        out=aRR[par][:, c0:TG], in0=aHT[par][:, c0:TG], scalar=NEGM, in1=PS[bc][:, c0:TG],
                        op0=ALU.mult, op1=ALU.add), [baHT[par], bPS[bc], bCOLS], [baRR[par]])

            def S4a(stp):
                c0, par = stp["c0"], stp["par"]
                b0 = stp["bz"][0]
                tr.op(act, lambda: nc.scalar.activation(out=aA_t[:, par, :, c0:TG], in_=PSALL[:, b0:b0 + 2, c0:TG], func=AF.Exp),
                      [bPS[b0], bPS[b0 + 1]], baA[par])

            def S5(stp):
                hp, qg, kb, K, c0, par = stp["hp"], stp["qg"], stp["kb"], stp["K"], stp["c0"], stp["par"]
                def g3():
                    ins = None
                    for x in range(2):
                        ins = nc.tensor.matmul(PS[bo][:, c0:TG], lhsT=VZ[:, kb, x, :], rhs=aA[par][x][:, c0:TG],
                                               start=(kb == K - 1 and x == 0), stop=(x == 1), skip_group_check=True)
                    return ins
                tr.op(pe, g3, [bVZ] + baA[par], [bPS[bo]])
                if stp["last"]:
                    tr.op(act, lambda: nc.scalar.activation(out=aOSQ[:, :], in_=PS[bo][:, :], func=AF.Square), [bPS[bo]], [bOSQ])
                    bs = stp["bz"][0]
                    tr.op(pe, lambda: nc.tensor.matmul(PS[bs][:, :], lhsT=BD[:, :], rhs=aOSQ[:, :], start=True, stop=True),
                          [bCONST, bOSQ], [bPS[bs]])
                    i_sd = t2()
                    tr.op(act, lambda: nc.scalar.activation(out=T2[i_sd][:, :], in_=PS[bs][:, :], func=AF.Ln, bias=EPS, scale=1.0 / HD),
                          [bPS[bs]], [bT2[i_sd]])
                    tr.op(act, lambda: nc.scalar.activation(out=T2[i_sd][:, :], in_=T2[i_sd][:, :], func=AF.Exp, scale=-0.5),
                          [bT2[i_sd]], [bT2[i_sd]])
                    i_t = t2()
                    tr.op(dve, lambda: nc.vector.scalar_tensor_tensor(
                        out=T2[i_t][:, :], in0=PS[bo][:, :], scalar=col(cb + 16), in1=T2[i_sd][:, :], op0=ALU.mult, op1=ALU.mult),
                        [bPS[bo], bT2[i_sd], bCOLS], [bT2[i_t]])
                    tr.op(dve, lambda: nc.vector.tensor_tensor(
                        out=SG[:, hp, qg * TG:(qg + 1) * TG], in0=T2[i_t][:, :], in1=SG[:, hp, qg * TG:(qg + 1) * TG], op=ALU.mult),
                        [bT2[i_t], bSG[hp][qg]], [bSG[hp][qg]])

            ns = len(steps)
            SV(0)
            S1(steps[0])
            for i in range(ns + 1):
                if i + 1 < ns:
                    S1(steps[i + 1])
                if i < ns:
                    S2(steps[i])
                    S3(steps[i])
                if i >= 1:
                    S4a(steps[i - 1])
                    S5(steps[i - 1])
                    if steps[i - 1]["last"] and steps[i - 1]["qg"] == 0 and steps[i - 1]["hp"] < 3:
                        SV(steps[i - 1]["hp"] + 1)

            last = (l == NL - 1)
            tail_r = {}

            def load_pt(tg):
                tr.op(pool, lambda: nc.gpsimd.dma_start(
                    out=PT2[tg % 2][:, :, :], in_=pT[l, :, tg * TG:(tg + 1) * TG].rearrange("(c p) n -> p c n", p=P)),
                    [], [bPT2[tg % 2]], chan=ch_p)

            HOOKS = (0, 2, 3, 4)

            def run_hosted(hosted, m):
                if m in HOOKS:
                    for g in hosted:
                        try:
                            next(g)
                        except StopIteration:
                            pass

            def flush(hosted):
                for g in hosted:
                    for _ in g:
                        pass

            def tail_A(tg, hosted=()):
                hs = tg % 2
                HBx, bHBx = HBs[hs], bHBs[hs]
                load_h(hsrc, tg, [bHS[tg]] if l > 0 else [], hs)
                for m in range(8):
                    run_hosted(hosted, m)
                    b = fm_matmul(WOUT, bWOUT, m, tg, rhs_fn=lambda k, tg=tg: SG[:, k, tg * TG:(tg + 1) * TG],
                                  rhs_bufs=[bSG[k][tg] for k in range(8)])
                    tr.op(dve, lambda b=b, m=m: nc.vector.tensor_tensor(out=HBx[:, m, :], in0=PS[b][:, :], in1=HBx[:, m, :], op=ALU.add),
                          [bPS[b]] + bHBx[m], bHBx[m])
                flush(hosted)

            def norm_gen(kind, tg):
                hs = tg % 2
                HBx, bHBx, SQx, bSQx = HBs[hs], bHBs[hs], SQs[hs], bSQs[hs]
                allb = lambda c0_: [u for c in range(c0_, c0_ + 4) for u in bHBx[c]]
                if kind == "B":
                    gcol0, out_fn = cb + 8, (lambda c: (HN2s[hs][:, c, :], bHN2s[hs][c]))
                elif not last:
                    gcol0, out_fn = (l + 1) * LC, (lambda c: (XR[:, c, tg * TG:(tg + 1) * TG], [bXR[c][tg]]))
                    dv = hscr[:, tg * TG:(tg + 1) * TG].rearrange("(c p) n -> p c n", p=P)
                    for c0_ in (0, 4):
                        tr.op(sp, lambda c0_=c0_: nc.sync.dma_start(out=dv[:, c0_:c0_ + 4, :], in_=HBx[:, c0_:c0_ + 4, :]),
                              allb(c0_), [bHS[tg]], chan=ch_h)
                else:
                    gcol0, out_fn = NL * LC, (lambda c: (HBx[:, c, :], bHBx[c]))
                for c in range(8):
                    tr.op(act, lambda c=c: nc.scalar.activation(out=SQx[:, c, :], in_=HBx[:, c, :], func=AF.Square), bHBx[c], bSQx[c])
                yield
                i_r = norm_stats(HBx, bHBx, SQx, bSQx, have_sq=True)
                yield
                for half in range(2):
                    for c in range(4 * half, 4 * half + 4):
                        oap, obufs = out_fn(c)
                        tr.op(dve, lambda c=c, oap=oap: nc.vector.scalar_tensor_tensor(
                            out=oap, in0=HBx[:, c, :], scalar=col(gcol0 + c), in1=T2[i_r][:, :], op0=ALU.mult, op1=ALU.mult),
                            bHBx[c] + [bT2[i_r], bCOLS], obufs)
                    if half == 0:
                        yield
                if kind == "D" and last:
                    dv = outT[:, tg * TG:(tg + 1) * TG].rearrange("(c p) n -> p c n", p=P)
                    for c0_ in (0, 4):
                        tr.op(sp, lambda c0_=c0_: nc.sync.dma_start(out=dv[:, c0_:c0_ + 4, :], in_=HBx[:, c0_:c0_ + 4, :]),
                              allb(c0_), [bOUT], chan=ch_o)

            def tail_C(tg, hosted=()):
                hs = tg % 2
                HBx, bHBx, HN2x, bHN2x = HBs[hs], bHBs[hs], HN2s[hs], bHN2s[hs]
                PT, bPT = PT2[tg % 2], bPT2[tg % 2]
                if tg == 0:
                    load_pt(0)
                if tg + 1 < NTG:
                    load_pt(tg + 1)
                for m in range(8):
                    run_hosted(hosted, m)
                    b1 = fm_matmul(WGATE, bWGATE, m, tg, rhs_fn=lambda k: HN2x[:, k, :], rhs_bufs=[u for k in range(8) for u in bHN2x[k]])
                    b2 = fm_matmul(WPLE, bWPLE, m, tg, nk=2, rhs_fn=lambda k: PT[:, k, :], rhs_bufs=[bPT])
                    i = t2()
                    tr.op(act, lambda b1=b1, i=i: nc.scalar.activation(out=T2[i][:, :], in_=PS[b1][:, :], func=AF.Sigmoid),
                          [bPS[b1]], [bT2[i]])
                    tr.op(dve, lambda b2=b2, i=i: nc.vector.tensor_tensor(out=T2[i][:, :], in0=PS[b2][:, :], in1=T2[i][:, :], op=ALU.mult),
                          [bPS[b2], bT2[i]], [bT2[i]])
                    tr.op(dve, lambda i=i, m=m: nc.vector.tensor_tensor(out=HBx[:, m, :], in0=HBx[:, m, :], in1=T2[i][:, :], op=ALU.add),
                          bHBx[m] + [bT2[i]], bHBx[m])
                flush(hosted)

            tail_A(0)
            gB0 = norm_gen("B", 0)
            next(gB0)
            tail_A(1, [gB0])
            tail_C(0, [norm_gen("B", 1)])
            tail_C(1, [norm_gen("D", 0)])
            tail_A(2, [norm_gen("D", 1)])
            gB2 = norm_gen("B", 2)
            next(gB2)
            tail_A(3, [gB2])
            tail_C(2, [norm_gen("B", 3)])
            tail_C(3, [norm_gen("D", 2)])
            flush([norm_gen("D", 3)])

        tr.wait_all(sp, ring_hw.chans + ring_sw.chans)
        tr.wait_all(sp, [pe.chan, act.chan, dve.chan, pool.chan])
    return nc


_NC = None


def _consts():
    c = np.zeros((P, C_END), np.float32)
    p = np.arange(P)[:, None]
    j = np.arange(P)[None, :]
    c[:, C_ID:C_ID + P] = (p == j)
    c[:, C_NEG:C_NEG + P] = np.where(p >= j, -30000.0, 0.0)
    c[:, C_NTRI:C_NTRI + P] = np.where(p >= j, -1.0, 0.0)
    c[:, C_SELA:C_SELA + P] = ((p == 0) | (p == 32)) * np.ones((1, P))
    c[:, C_SELB:C_SELB + P] = ((p == 64) | (p == 96)) * np.ones((1, P))
    c[:, C_BD:C_BD + P] = ((p // 64) == (j // 64))
    c[:, C_NONA:C_NONA + 64] = -1.0
    c[:, C_NONB + 64:C_NONB + P] = -1.0
    return c


def _cols(norm_g, ple_norm_g, attn_out_g, dw_w, dw_b, conv_ln_g, conv_ln_b, conv_out_g, final_g):
    c = np.zeros((P, NCOL), np.float32)
    for l in range(NL):
        b = l * LC
        c[:, b:b + 8] = norm_g[l].reshape(8, P).T
        c[:, b + 8:b + 16] = ple_norm_g[l].reshape(8, P).T
        c[:, b + 16] = np.tile(attn_out_g[l], 2)
        c[:, b + 17:b + 21] = dw_b[l].reshape(4, P).T
        c[:, b + 21:b + 25] = conv_ln_g[l].reshape(4, P).T
        c[:, b + 25:b + 29] = conv_ln_b[l].reshape(4, P).T
        c[:, b + 29:b + 33] = conv_out_g[l].reshape(4, P).T
        c[:, b + 33:b + 33 + 4 * CW] = dw_w[l].reshape(CW, 4, P).transpose(2, 1, 0).reshape(P, 4 * CW)
    c[:, NL * LC:NL * LC + 8] = final_g.reshape(8, P).T
    c[:, NL * LC + 8] = np.where((np.arange(P) // 32) % 2 == 1, -1.0, 0.0)
    return c


def kernel(x, p, norm_g, w_in, attn_out_g, dw_w, dw_b, conv_ln_g, conv_ln_b, w_pw, conv_out_g, w_out,
           ple_norm_g, w_ple_gate, w_ple, final_g):
    global _NC
    f = lambda a: np.ascontiguousarray(np.asarray(a, dtype=np.float32))
    x = f(x)
    p = f(p)
    B = x.shape[0]
    cols = _cols(f(norm_g), f(ple_norm_g), f(attn_out_g), f(dw_w), f(dw_b), f(conv_ln_g), f(conv_ln_b), f(conv_out_g), f(final_g))
    cst = _consts()
    shared = {"w_in": f(w_in), "w_pw": f(w_pw), "w_out": f(w_out), "w_gate": f(w_ple_gate), "w_ple": f(w_ple),
              "cols": cols, "cst": cst}
    in_maps = []
    for b in range(B):
        m = dict(shared)
        m["xT"] = np.ascontiguousarray(x[b].T)
        m["pT"] = np.ascontiguousarray(p[:, b].transpose(0, 2, 1))
        in_maps.append(m)
    if _NC is None:
        _NC = build()
    res = run_bass_kernel_spmd(_NC, in_maps, core_ids=list(range(B)))
    out = np.stack([np.ascontiguousarray(r["outT"].T) for r in res.results], axis=0)
    return out.astype(np.float32)
```

```python
import contextlib
import numpy as np
import concourse.bass as bass
import concourse.mybir as mybir
from concourse.bass_utils import run_bass_kernel_spmd

F32 = mybir.dt.float32
BF16 = mybir.dt.bfloat16
F16 = mybir.dt.float16
AF = mybir.ActivationFunctionType
ALU = mybir.AluOpType

P = 128
S = 2048
D = 1024
NL = 2
DIN = 3584
NH = 8
HD = 64
CW = 31
PLE = 256
TG = 512
NTG = S // TG
EPS = 1e-6
LC = 157
NCOL = NL * LC + 9
C_ID, C_NEG, C_NTRI, C_SELA, C_SELB, C_BD, C_NONA, C_NONB, C_END = 0, 128, 256, 384, 512, 640, 768, 896, 1024


class Buf:
    __slots__ = ("name", "w", "r")

    def __init__(self, name):
        self.name = name
        self.w = None
        self.r = {}


class Chan:
    def __init__(self, sem, step):
        self.sem = sem
        self.count = 0
        self.step = step


class Ring:
    def __init__(self, chans):
        self.chans = chans
        self.i = 0

    def next(self):
        c = self.chans[self.i % len(self.chans)]
        self.i += 1
        return c


class Eng:
    def __init__(self, raw, chan, safe):
        self.raw = raw
        self.chan = chan
        self.known = {}
        self.safe = safe


class T:
    def __init__(self, nc, stack):
        self.nc = nc
        self.stack = stack
        self.nsem = 0
        mk = self.chan
        self.pe = Eng(nc.tensor, mk(1), True)
        self.act = Eng(nc.scalar, mk(1), False)
        self.dve = Eng(nc.vector, mk(1), False)
        self.pool = Eng(nc.gpsimd, mk(1), False)
        self.sp = Eng(nc.sync, mk(1), False)

    def chan(self, step=16):
        self.nsem += 1
        sem = self.stack.enter_context(self.nc.semaphore("sm%d" % self.nsem))
        return Chan(sem, step)

    def op(self, eng, fn, reads=(), writes=(), chan=None):
        deps = {}

        def add(d):
            if d is None:
                return
            c, v = d
            if deps.get(c, 0) < v:
                deps[c] = v

        for b in reads:
            add(b.w)
        for b in writes:
            add(b.w)
            for c, v in b.r.items():
                add((c, v))
        if isinstance(chan, Ring):
            chan = chan.next()
            if chan.count > 0:
                deps[chan] = max(deps.get(chan, 0), chan.count)
        target = chan or eng.chan
        for c, v in deps.items():
            if c is eng.chan and eng.safe and chan is None:
                continue
            if eng.known.get(c, 0) >= v:
                continue
            eng.raw.wait_ge(c.sem, v)
            eng.known[c] = v
        ins = fn()
        target.count += target.step
        ins.then_inc(target.sem, target.step)
        for b in writes:
            b.w = (target, target.count)
            b.r = {}
        for b in reads:
            if b.r.get(target, 0) < target.count:
                b.r[target] = target.count
        return ins

    def wait_all(self, eng, chans):
        for c in chans:
            if c.count > 0:
                eng.raw.wait_ge(c.sem, c.count)


def build():
    nc = bass.Bass("TRN2", target_bir_lowering=False)
    dt = nc.dram_tensor
    xT = dt("xT", [D, S], F32, kind="ExternalInput").ap()
    pT = dt("pT", [NL, PLE, S], F32, kind="ExternalInput").ap()
    w_in = dt("w_in", [NL, D, DIN], F32, kind="ExternalInput").ap()
    w_pw = dt("w_pw", [NL, 512, 512], F32, kind="ExternalInput").ap()
    w_out = dt("w_out", [NL, D, D], F32, kind="ExternalInput").ap()
    w_gate = dt("w_gate", [NL, D, D], F32, kind="ExternalInput").ap()
    w_ple = dt("w_ple", [NL, PLE, D], F32, kind="ExternalInput").ap()
    cols_d = dt("cols", [P, NCOL], F32, kind="ExternalInput").ap()
    cst_d = dt("cst", [P, C_END], F32, kind="ExternalInput").ap()
    outT = dt("outT", [D, S], F32, kind="ExternalOutput").ap()
    hscr = dt("hscr", [D, S], F32, kind="Internal").ap()
    dgs = dt("dgs", [4, P, CW * P], BF16, kind="Internal").ap()

    with contextlib.ExitStack() as st:
        tr = T(nc, st)
        pe, act, dve, pool, sp = tr.pe, tr.act, tr.dve, tr.pool, tr.sp

        def sb(name, shape, dtype):
            return st.enter_context(nc.sbuf_tensor(name, shape, dtype))

        XR = sb("XR", [P, 8, S], BF16)
        SG = sb("SG", [P, 8, S], BF16)
        QT = sb("QT", [P, 4, S], BF16)
        KT = sb("KT", [P, 4, S], BF16)
        VV = sb("VV", [P, 16, 512], BF16)
        CP = sb("CP", [P, 4, S + 32], BF16)
        WT = sb("WT", [P, 18432], BF16)
        PT2 = [sb("PTt%d" % i, [P, 2, TG], BF16) for i in range(2)]
        T2 = [sb("T2_%d" % i, [P, TG], F32) for i in range(3)]
        AW = sb("AW", [P, 8192], BF16)
        aE_t = AW[:, 0:4096].bitcast(F32).rearrange("p (a b t) -> p a b t", a=2, b=2)
        aE = [[aE_t[:, i, j, :] for j in range(2)] for i in range(2)]
        aSP_t = AW[:, 4096:6144].bitcast(F16).rearrange("p (a b t) -> p a b t", a=2, b=2)
        aSP = [[aSP_t[:, i, j, :] for j in range(2)] for i in range(2)]
        aA_t = AW[:, 6144:8192].rearrange("p (a b t) -> p a b t", a=2, b=2)
        DGB = [AW[:, 0:CW * P].rearrange("p (w j) -> p w j", w=CW), AW[:, 4096:4096 + CW * P].rearrange("p (w j) -> p w j", w=CW)]
        DGBF = [AW[:, 0:CW * P], AW[:, 4096:4096 + CW * P]]
        aA = [[aA_t[:, i, j, :] for j in range(2)] for i in range(2)]
        aHT = [sb("aHT%d" % i, [P, TG], BF16) for i in range(2)]
        aRR = [sb("aRR%d" % i, [P, TG], BF16) for i in range(2)]
        aOSQ = sb("aOSQ", [P, TG], BF16)
        COLS = sb("COLS", [P, NCOL], F32)
        IDB = sb("IDB", [P, P], BF16)
        NEGB = sb("NEGB", [P, P], BF16)
        NTRI = sb("NTRI", [P, P], F16)
        SELA = sb("SELA", [P, P], BF16)
        SELB = sb("SELB", [P, P], BF16)
        BD = sb("BD", [P, P], BF16)
        NONA = sb("NONA", [P, P], F16)
        NONB = sb("NONB", [P, P], F16)
        VZ = sb("VZ", [P, 16, 2, P], BF16)
        ONESB = sb("ONESB", [P, P], BF16)
        NEGM = COLS[:, NL * LC + 8:NL * LC + 9]
        PSALL = st.enter_context(nc.psum_tensor("psall", [P, 8, TG], F32))
        PS = [PSALL[:, i, :] for i in range(8)]

        bXR = [[Buf("xr%d_%d" % (c, t)) for t in range(NTG)] for c in range(8)]
        bSG = [[Buf("sg%d_%d" % (c, t)) for t in range(NTG)] for c in range(8)]
        bQ = [[Buf("q%d_%d" % (c, t)) for t in range(NTG)] for c in range(4)]
        bK = [[Buf("k%d_%d" % (c, t)) for t in range(NTG)] for c in range(4)]
        bV = [Buf("v%d" % i) for i in range(16)]
        bCP = [[Buf("cp%d_%d" % (c, t)) for t in range(NTG)] for c in range(4)]
        bWT = [Buf("wt%d" % i) for i in range(4)]
        bPT2 = [Buf("pt0"), Buf("pt1")]
        bDGB = [[Buf("dgb") for w in range(CW)] for _ in range(2)]
        bDGS = [Buf("dgs%d" % i) for i in range(4)]
        bT2 = [Buf("t2_%d" % i) for i in range(3)]
        baE = [[Buf("e") for j in range(2)] for i in range(2)]
        baSP = [[Buf("sp") for j in range(2)] for i in range(2)]
        baA = [[Buf("a") for j in range(2)] for i in range(2)]
        baHT = [Buf("ht") for i in range(2)]
        baRR = [Buf("rr") for i in range(2)]
        bOSQ = Buf("osq")
        bCONST = Buf("const")
        bCOLS = Buf("cols")
        bPS = [Buf("ps%d" % i) for i in range(8)]
        bHS = [Buf("hscr%d" % t) for t in range(NTG)]
        bOUT = Buf("out")

        XRF = XR[:, :, :].rearrange("p c s -> p (c s)")
        QTF = QT[:, :, :].rearrange("p c s -> p (c s)")
        KTF = KT[:, :, :].rearrange("p c s -> p (c s)")
        VVF = VV[:, :, :].rearrange("p c s -> p (c s)")
        CPF = CP[:, :, :].rearrange("p c s -> p (c s)")
        QB = QTF.bitcast(F32).rearrange("p (c t) -> p c t", c=8)
        bQB = [[bQ[c // 2][(c % 2) * 2], bQ[c // 2][(c % 2) * 2 + 1]] for c in range(8)]
        KB = KTF.bitcast(F32).rearrange("p (c t) -> p c t", c=8)
        bKB = [[bK[c // 2][(c % 2) * 2], bK[c // 2][(c % 2) * 2 + 1]] for c in range(8)]
        VB = VVF.bitcast(F32).rearrange("p (c t) -> p c t", c=8)
        bVB = [[bV[2 * c], bV[2 * c + 1]] for c in range(8)]

        def cp_units(a0, a1):
            out = []
            for cc in range(4):
                lo, hi = max(a0, cc * (S + 32)), min(a1, (cc + 1) * (S + 32))
                if lo < hi:
                    for t in range(min(3, (lo - cc * (S + 32)) // 512), min(3, (hi - 1 - cc * (S + 32)) // 512) + 1):
                        if bCP[cc][t] not in out:
                            out.append(bCP[cc][t])
            return out

        HBs = [QB, VB]
        bHBs = [bQB, bVB]
        SQs = [KTF[:, 0:4096].rearrange("p (c t) -> p c t", c=8), CPF[:, 0:4096].rearrange("p (c t) -> p c t", c=8)]
        bSQs = [[[bK[c // 4][c % 4]] for c in range(8)], [cp_units(c * 512, (c + 1) * 512) for c in range(8)]]
        HN2s = [KTF[:, 4096:8192].rearrange("p (c t) -> p c t", c=8), CPF[:, 4096:8192].rearrange("p (c t) -> p c t", c=8)]
        bHN2s = [[[bK[2 + c // 4][c % 4]] for c in range(8)], [cp_units(4096 + c * 512, 4096 + (c + 1) * 512) for c in range(8)]]
        ACCs = [QB[:, 0:4, :], QB[:, 4:8, :]]
        bACCs = [bQB[0:4], bQB[4:8]]
        C2s = [KB[:, 0:4, :], KB[:, 4:8, :]]
        bC2s = [bKB[0:4], bKB[4:8]]
        ACBs = [VV[:, 0:4, :], VV[:, 4:8, :]]
        bACBs = [[[bV[i]] for i in range(0, 4)], [[bV[i]] for i in range(4, 8)]]
        SQCs = [VV[:, 8:12, :], VV[:, 12:16, :]]
        bSQCs = [[[bV[i]] for i in range(8, 12)], [[bV[i]] for i in range(12, 16)]]

        def xr_units(u0, n):
            return [bXR[u // 4][u % 4] for u in range(u0, u0 + n)]

        WPW = WT[:, 12288:14336].rearrange("p (c n) -> p c n", c=4)
        bWPW = bWT[3]
        QZ = [[XRF[:, (28 + 2 * a + x) * 512:(29 + 2 * a + x) * 512] for x in range(2)] for a in range(2)]
        bQZ = [[xr_units(28 + 2 * a + x, 1)[0] for x in range(2)] for a in range(2)]
        bVZ = Buf("vz")
        ring_sw = Ring([tr.chan() for _ in range(6)])
        ring_hw = Ring([tr.chan() for _ in range(6)])
        ch_w = ch_c = ch_p = ring_sw
        ch_h = ch_o = ring_hw

        def col(i):
            return COLS[:, i:i + 1]

        tr.op(sp, lambda: nc.sync.dma_start(out=COLS[:, :], in_=cols_d[:, :]), [], [bCOLS], chan=ch_h)
        bONES = Buf("ones")
        tr.op(dve, lambda: nc.vector.memset(ONESB[:, :], 1.0), [], [bONES])

        def load_consts():
            for dst, c0, c1 in ((IDB, C_ID, C_NEG), (NEGB, C_NEG, C_NTRI), (NTRI, C_NTRI, C_SELA), (SELA, C_SELA, C_SELB),
                                (SELB, C_SELB, C_BD), (BD, C_BD, C_NONA), (NONA, C_NONA, C_NONB), (NONB, C_NONB, C_END)):
                tr.op(pool, lambda dst=dst, c0=c0, c1=c1: nc.gpsimd.dma_start(out=dst[:, :], in_=cst_d[:, c0:c1]),
                      [], [bCONST], chan=ch_c)
            tr.op(pool, lambda: nc.gpsimd.memset(VZ[:, :, :, :], 0.0), [], [bVZ])

        gen_rr = [0]

        def gbank():
            i = gen_rr[0] % 8
            gen_rr[0] += 1
            return i

        t2_rr = [0]

        def t2():
            i = t2_rr[0] % 3
            t2_rr[0] += 1
            return i

        def norm_stats(hb_ap, hb_bufs, sq_ap, sq_bufs, nchunk=8, inv_n=1.0 / D, have_sq=False):
            for c in range(0 if have_sq else nchunk):
                tr.op(act, lambda c=c: nc.scalar.activation(out=sq_ap[:, c, :], in_=hb_ap[:, c, :], func=AF.Square),
                      hb_bufs[c], sq_bufs[c])
            b = gbank()

            def mm():
                ins = None
                for c in range(nchunk):
                    ins = nc.tensor.matmul(PS[b][:, :], lhsT=ONESB[:, :], rhs=sq_ap[:, c, :], start=(c == 0), stop=(c == nchunk - 1))
                return ins
            tr.op(pe, mm, [bONES] + [u for c in range(nchunk) for u in sq_bufs[c]], [bPS[b]])
            i_r = t2()
            tr.op(act, lambda: nc.scalar.activation(out=T2[i_r][:, :], in_=PS[b][:, :], func=AF.Ln, bias=EPS, scale=inv_n),
                  [bPS[b]], [bT2[i_r]])
            tr.op(act, lambda: nc.scalar.activation(out=T2[i_r][:, :], in_=T2[i_r][:, :], func=AF.Exp, scale=-0.5),
                  [bT2[i_r]], [bT2[i_r]])
            return i_r

        def norm_apply(hb_ap, hb_bufs, gcol0, i_r, out_fn, nchunk=8):
            for c in range(nchunk):
                oap, obufs = out_fn(c)
                tr.op(dve, lambda c=c, oap=oap: nc.vector.scalar_tensor_tensor(
                    out=oap, in0=hb_ap[:, c, :], scalar=col(gcol0 + c), in1=T2[i_r][:, :], op0=ALU.mult, op1=ALU.mult),
                    hb_bufs[c] + [bT2[i_r], bCOLS], obufs)

        def load_h(src, tg, src_bufs, hs):
            sv = src[:, tg * TG:(tg + 1) * TG].rearrange("(c p) n -> p c n", p=P)
            for c0_ in (0, 4):
                tr.op(sp, lambda c0_=c0_: nc.sync.dma_start(out=HBs[hs][:, c0_:c0_ + 4, :], in_=sv[:, c0_:c0_ + 4, :]),
                      src_bufs, [u for c in range(c0_, c0_ + 4) for u in bHBs[hs][c]], chan=ch_h)

        load_h(xT, 0, [], 0)
        load_h(xT, 1, [], 1)
        for tg in range(NTG):
            hs = tg % 2
            i_r = norm_stats(HBs[hs], bHBs[hs], SQs[hs], bSQs[hs])
            norm_apply(HBs[hs], bHBs[hs], 0, i_r, lambda c, tg=tg: (XR[:, c, tg * TG:(tg + 1) * TG], [bXR[c][tg]]))
            if tg + 2 < NTG:
                load_h(xT, tg + 2, [], hs)

        for l in range(NL):
            cb = l * LC
            hsrc = xT if l == 0 else hscr
            wslot = [0]

            def load_w(dram_ap, nk, ncols, slots, col0=0):
                n = nk * ncols
                base = slots[0] * 4096 + col0
                dst = WT[:, base:base + n].rearrange("p (c n) -> p c n", c=nk)
                src = dram_ap.rearrange("(c p) n -> p c n", p=P)
                step = max(1, 2048 // ncols)
                for k0 in range(0, nk, step):
                    k1 = min(nk, k0 + step)
                    tr.op(pool, lambda k0=k0, k1=k1: nc.gpsimd.dma_start(out=dst[:, k0:k1, :], in_=src[:, k0:k1, :]),
                          [], [bWT[s_] for s_ in slots], chan=ch_w)
                return dst

            def win_group(j):
                s_ = wslot[0] % 3
                wslot[0] += 1
                return load_w(w_in[l, :, j * 512:(j + 1) * 512], 8, 512, [s_]), bWT[s_]

            def fm_matmul(w_ap, wbuf, m, tg, nk=8, rhs_fn=None, rhs_bufs=None):
                b = gbank()
                if rhs_fn is None:
                    rhs_fn = lambda k: XR[:, k, tg * TG:(tg + 1) * TG]
                    rhs_bufs = [bXR[k][tg] for k in range(nk)]

                def mm():
                    ins = None
                    for k in range(nk):
                        ins = nc.tensor.matmul(PS[b][:, :], lhsT=w_ap[:, k, m * P:(m + 1) * P], rhs=rhs_fn(k),
                                               start=(k == 0), stop=(k == nk - 1))
                    return ins
                tr.op(pe, mm, wbuf + rhs_bufs, [bPS[b]])
                return b

            wv, bv = win_group(4)
            wg, bg = win_group(5)
            wq, bq = win_group(6)
            for c in range(4):
                tr.op(pool, lambda c=c: nc.gpsimd.memset(CP[:, c, 0:32], 0.0), [], [bCP[c][0]])
            if l == 0:
                load_consts()
            for tg in range(NTG):
                for m in range(4):
                    b1 = fm_matmul(wv, [bv], m, tg)
                    b2 = fm_matmul(wg, [bg], m, tg)
                    i = t2()
                    tr.op(act, lambda b2=b2, i=i: nc.scalar.activation(out=T2[i][:, :], in_=PS[b2][:, :], func=AF.Sigmoid),
                          [bPS[b2]], [bT2[i]])
                    tr.op(dve, lambda b1=b1, i=i, m=m, tg=tg: nc.vector.tensor_tensor(
                        out=CP[:, m, 30 + tg * TG:30 + (tg + 1) * TG], in0=PS[b1][:, :], in1=T2[i][:, :], op=ALU.mult),
                        [bPS[b1], bT2[i]], [bCP[m][tg]] + ([bCP[m][tg + 1]] if tg + 1 < NTG else []))
            for tg in range(NTG):
                for m in range(4):
                    b1 = fm_matmul(wq, [bq], m, tg)
                    tr.op(act, lambda b1=b1, m=m, tg=tg: nc.scalar.activation(
                        out=SG[:, 4 + m, tg * TG:(tg + 1) * TG], in_=PS[b1][:, :], func=AF.Silu), [bPS[b1]], [bSG[4 + m][tg]])
            tr.op(pool, lambda: nc.gpsimd.dma_start(out=WPW[:, :, :], in_=w_pw[l].rearrange("(c p) n -> p c n", p=P)),
                  [], [bWPW], chan=ch_w)
            wq0 = win_group(0)
            wq1 = win_group(1)

            dg_rr = [0]

            prepped = set()

            def conv_prep(tg, ch):
                if (tg, ch) in prepped or tg >= NTG:
                    return
                prepped.add((tg, ch))
                rb_ = (tg * 4 + ch) % 2
                if tg == 0:
                    for w in range(CW):
                        if w % 3 == 2:
                            tr.op(act, lambda w=w: nc.scalar.activation(
                                out=DGB[rb_][:, w, :], in_=IDB[:, :], func=AF.Copy, scale=col(cb + 33 + ch * CW + w)),
                                [bCONST, bCOLS], [bDGB[rb_][w]])
                        else:
                            tr.op(dve, lambda w=w: nc.vector.tensor_scalar(
                                out=DGB[rb_][:, w, :], in0=IDB[:, :], scalar1=col(cb + 33 + ch * CW + w), scalar2=None, op0=ALU.mult),
                                [bCONST, bCOLS], [bDGB[rb_][w]])
                    tr.op(sp, lambda: nc.sync.dma_start(out=dgs[ch, :, :], in_=DGBF[rb_]), bDGB[rb_], [bDGS[ch]], chan=ch_h)
                else:
                    tr.op(sp, lambda: nc.sync.dma_start(out=DGBF[rb_], in_=dgs[ch, :, :]), [bDGS[ch]], bDGB[rb_], chan=ch_h)

            def conv_taps(tg, cs_):
                ACC, bACC, ACB, bACB, SQC, bSQC = ACCs[cs_], bACCs[cs_], ACBs[cs_], bACBs[cs_], SQCs[cs_], bSQCs[cs_]
                conv_prep(tg, 0)
                for ch in range(4):
                    if ch > 0:
                        yield
                    if ch < 3:
                        conv_prep(tg, ch + 1)
                    else:
                        conv_prep(tg + 1, 0)
                    rb_ = (tg * 4 + ch) % 2
                    b = gbank()
                    rb = [bCP[ch][tg]] + ([bCP[ch][tg + 1]] if tg + 1 < NTG else [])
                    for w in range(CW):
                        if w == 15:
                            yield
                        tr.op(pe, lambda w=w: nc.tensor.matmul(
                            PS[b][:, :], lhsT=DGB[rb_][:, w, :], rhs=CP[:, ch, tg * TG + w:tg * TG + w + TG],
                            start=(w == 0), stop=(w == CW - 1)), [bDGB[rb_][w]] + rb, [bPS[b]])
                    bia = col(cb + 17 + ch)
                    tr.op(act, lambda: nc.scalar.activation(
                        out=ACC[:, ch, :], in_=PS[b][:, :], func=AF.Identity, bias=bia), [bPS[b], bCOLS], bACC[ch])
                    tr.op(pool, lambda: nc.gpsimd.tensor_copy(out=ACB[:, ch, :], in_=ACC[:, ch, :]), bACC[ch], bACB[ch])
                    tr.op(act, lambda: nc.scalar.activation(out=SQC[:, ch, :], in_=ACC[:, ch, :], func=AF.Square), bACC[ch], bSQC[ch])

            def conv_post(tg, cs_):
                ACC, bACC, ACB, bACB, SQC, bSQC = ACCs[cs_], bACCs[cs_], ACBs[cs_], bACBs[cs_], SQCs[cs_], bSQCs[cs_]
                C2, bC2 = C2s[cs_], bC2s[cs_]
                CS, bCS = ACB, bACB
                bm = gbank()

                def mm_mean():
                    ins = None
                    for ch in range(4):
                        ins = nc.tensor.matmul(PS[bm][:, :], lhsT=ONESB[:, :], rhs=ACB[:, ch, :], start=(ch == 0), stop=(ch == 3))
                    return ins
                tr.op(pe, mm_mean, [bCONST] + [u for ch in range(4) for u in bACB[ch]], [bPS[bm]])
                bq2 = gbank()

                def mm_sq():
                    ins = None
                    for ch in range(4):
                        ins = nc.tensor.matmul(PS[bq2][:, :], lhsT=ONESB[:, :], rhs=SQC[:, ch, :], start=(ch == 0), stop=(ch == 3))
                    return ins
                tr.op(pe, mm_sq, [bCONST] + [u for ch in range(4) for u in bSQC[ch]], [bPS[bq2]])
                i_mean = t2()
                tr.op(act, lambda: nc.scalar.mul(out=T2[i_mean][:, :], in_=PS[bm][:, :], mul=1.0 / 512), [bPS[bm]], [bT2[i_mean]])
                i_var = t2()
                tr.op(dve, lambda: nc.vector.tensor_tensor(out=T2[i_var][:, :], in0=T2[i_mean][:, :], in1=T2[i_mean][:, :], op=ALU.mult),
                      [bT2[i_mean]], [bT2[i_var]])
                tr.op(dve, lambda: nc.vector.scalar_tensor_tensor(
                    out=T2[i_var][:, :], in0=PS[bq2][:, :], scalar=1.0 / 512, in1=T2[i_var][:, :], op0=ALU.mult, op1=ALU.subtract),
                    [bPS[bq2], bT2[i_var]], [bT2[i_var]])
                tr.op(act, lambda: nc.scalar.activation(out=T2[i_var][:, :], in_=T2[i_var][:, :], func=AF.Ln, bias=EPS, scale=1.0),
                      [bT2[i_var]], [bT2[i_var]])
                tr.op(act, lambda: nc.scalar.activation(out=T2[i_var][:, :], in_=T2[i_var][:, :], func=AF.Exp, scale=-0.5),
                      [bT2[i_var]], [bT2[i_var]])
                for ch in range(4):
                    tr.op(pool, lambda ch=ch: nc.gpsimd.tensor_tensor(out=ACC[:, ch, :], in0=ACC[:, ch, :], in1=T2[i_mean][:, :], op=ALU.subtract),
                          bACC[ch] + [bT2[i_mean]], bACC[ch])
                    tr.op(dve, lambda ch=ch: nc.vector.tensor_tensor(out=ACC[:, ch, :], in0=ACC[:, ch, :], in1=T2[i_var][:, :], op=ALU.mult),
                          bACC[ch] + [bT2[i_var]], bACC[ch])
                    tr.op(act, lambda ch=ch: nc.scalar.activation(
                        out=CS[:, ch, :], in_=ACC[:, ch, :], func=AF.Silu, bias=col(cb + 25 + ch), scale=col(cb + 21 + ch)),
                        bACC[ch] + [bCOLS], bCS[ch])
                yield
                for m in range(4):
                    b = fm_matmul(WPW, [bWPW], m, tg, nk=4, rhs_fn=lambda k: CS[:, k, :],
                                  rhs_bufs=[u for k in range(4) for u in bCS[k]])
                    tr.op(act, lambda b=b, m=m: nc.scalar.copy(out=C2[:, m, :], in_=PS[b][:, :]), [bPS[b]], bC2[m])
                    tr.op(act, lambda b=b, m=m: nc.scalar.activation(out=SQC[:, m, :], in_=PS[b][:, :], func=AF.Square), [bPS[b]], bSQC[m])
                yield
                i_r = norm_stats(C2, bC2, SQC, bSQC, nchunk=4, inv_n=1.0 / 512, have_sq=True)
                norm_apply(C2, bC2, cb + 29, i_r, lambda m: (C2[:, m, :], bC2[m]), nchunk=4)
                for m in range(4):
                    tr.op(dve, lambda m=m: nc.vector.tensor_tensor(
                        out=SG[:, 4 + m, tg * TG:(tg + 1) * TG], in0=C2[:, m, :], in1=SG[:, 4 + m, tg * TG:(tg + 1) * TG], op=ALU.mult),
                        bC2[m] + [bSG[4 + m][tg]], [bSG[4 + m][tg]])

            def wq_chunk(m):
                wq, bq = wq0
                for tg in range(NTG):
                    b1 = fm_matmul(wq, [bq], m, tg)
                    tr.op(dve, lambda b1=b1, tg=tg: nc.vector.tensor_scalar(
                        out=QT[:, m, tg * TG:(tg + 1) * TG], in0=PS[b1][:, :], scalar1=HD ** -0.5, scalar2=None, op0=ALU.mult),
                        [bPS[b1]], [bQ[m][tg]])

            def wk_chunk(m):
                wq, bq = wq1
                for tg in range(NTG):
                    b1 = fm_matmul(wq, [bq], m, tg)
                    tr.op(dve, lambda b1=b1, tg=tg: nc.vector.tensor_copy(
                        out=KT[:, m, tg * TG:(tg + 1) * TG], in_=PS[b1][:, :]), [bPS[b1]], [bK[m][tg]])

            def last_fill():
                wq_chunk(0)
                yield
                wq_chunk(1)
                yield
                wk_chunk(0)
                yield
                wk_chunk(1)

            def interleave(ga, gb, points):
                seg = 0
                la, lb = True, True
                while la:
                    try:
                        next(ga)
                    except StopIteration:
                        la = False
                    if seg in points and lb:
                        try:
                            next(gb)
                        except StopIteration:
                            lb = False
                    seg += 1
                while lb:
                    try:
                        next(gb)
                    except StopIteration:
                        lb = False

            for _ in conv_taps(0, 0):
                pass
            for tg in range(NTG):
                if tg + 1 < NTG:
                    interleave(conv_taps(tg + 1, (tg + 1) % 2), conv_post(tg, tg % 2), (0, 3, 5))
                else:
                    interleave(last_fill(), conv_post(tg, tg % 2), (0, 1, 2))

            wq_chunk(2)
            wq_chunk(3)
            wk_chunk(2)
            wk_chunk(3)
            wq, bq = win_group(2)
            wg3 = win_group(3)
            for tb in range(16):
                b = gbank()

                def mm(b=b, tb=tb, wq=wq):
                    ins = None
                    for k in range(8):
                        ins = nc.tensor.matmul(PS[b][:, :], lhsT=XR[:, k, tb * P:(tb + 1) * P], rhs=wq[:, k, :],
                                               start=(k == 0), stop=(k == 7))
                    return ins
                tr.op(pe, mm, [bq] + [bXR[k][tb // 4] for k in range(8)], [bPS[b]])
                tr.op(act, lambda b=b, tb=tb: nc.scalar.copy(out=VV[:, tb, :], in_=PS[b][:, :]), [bPS[b]], [bV[tb]])
            wq, bq = wg3
            for m in range(4):
                for tg in range(NTG):
                    b1 = fm_matmul(wq, [bq], m, tg)
                    tr.op(act, lambda b1=b1, m=m, tg=tg: nc.scalar.activation(
                        out=SG[:, m, tg * TG:(tg + 1) * TG], in_=PS[b1][:, :], func=AF.Silu), [bPS[b1]], [bSG[m][tg]])

            WOUT = load_w(w_out[l], 8, 1024, [0, 1])
            bWOUT = [bWT[0], bWT[1]]
            WGATE = load_w(w_gate[l], 8, 1024, [2, 3])
            bWGATE = [bWT[2], bWT[3]]
            WPLE = load_w(w_ple[l], 2, 1024, [3], col0=4096)
            bWPLE = [bWT[3]]

            steps = []
            for hp in range(4):
                for qg in (3, 2, 1, 0):
                    K = 4 * qg + 4
                    for kb in range(K - 1, -1, -1):
                        steps.append(dict(hp=hp, qg=qg, kb=kb, K=K, c0=(0 if kb < 4 * qg else P * (kb - 4 * qg)),
                                          diag=(kb >= 4 * qg), first=(kb == K - 1), last=(kb == 0)))
            sidx = -1
            for i, stp in enumerate(steps):
                if stp["first"]:
                    sidx += 1
                stp["sp"] = sidx % 2
                stp["c0_prev"] = None if stp["first"] else steps[i - 1]["c0"]
                stp["bz"] = [(i % 3) * 2, (i % 3) * 2 + 1]
                stp["par"] = i % 2
            bo, bc = 6, 7
            prs = [(0, 64), (64, 128)]

            for a in range(2):
                tr.op(dve, lambda a=a: nc.vector.memset(QZ[a][0][64:128, :], 0.0), [], [bQZ[a][0]])
                tr.op(dve, lambda a=a: nc.vector.memset(QZ[a][1][0:64, :], 0.0), [], [bQZ[a][1]])

            def S0(stp):
                hp, qg, sp_ = stp["hp"], stp["qg"], stp["sp"]
                q0 = qg * TG
                for x in range(2):
                    p0, p1 = prs[x]
                    tr.op(dve, lambda x=x, p0=p0, p1=p1: nc.vector.tensor_copy(out=QZ[sp_][x][p0:p1, :], in_=QT[p0:p1, hp, q0:q0 + TG]),
                          [bQ[hp][qg]], [bQZ[sp_][x]])

            def SV(hp):
                for x in range(2):
                    h = 2 * hp + x
                    tr.op(dve, lambda x=x, h=h: nc.vector.tensor_copy(out=VZ[:, :, x, x * HD:(x + 1) * HD], in_=VV[:, :, h * HD:(h + 1) * HD]),
                          bV, [bVZ])

            def S1(stp):
                hp, qg, kb, c0, diag, sp_ = stp["hp"], stp["qg"], stp["kb"], stp["c0"], stp["diag"], stp["sp"]
                if stp["first"]:
                    S0(stp)
                q0 = qg * TG
                def g1():
                    ins = None
                    for x in range(2):
                        bzx = stp["bz"][x]
                        ins = nc.tensor.matmul(PS[bzx][:, c0:TG], lhsT=KT[:, hp, kb * P:(kb + 1) * P],
                                               rhs=QZ[sp_][x][:, c0:TG], start=True, stop=True)
                        if diag:
                            ins = nc.tensor.matmul(PS[bzx][:, c0:c0 + P], lhsT=IDB[:, :], rhs=NEGB[:, :], start=False, stop=True,
                                                   skip_group_check=True)
                    return ins
                tr.op(pe, g1, [bK[hp][kb // 4], bQZ[sp_][0], bQZ[sp_][1], bCONST], [bPS[stp["bz"][0]], bPS[stp["bz"][1]]])

            def S2(stp):
                c0, par = stp["c0"], stp["par"]
                b0 = stp["bz"][0]
                tr.op(act, lambda: nc.scalar.activation(out=aE_t[:, par, :, c0:TG], in_=PSALL[:, b0:b0 + 2, c0:TG], func=AF.Exp),
                      [bPS[b0], bPS[b0 + 1]], baE[par])
                tr.op(act, lambda: nc.scalar.activation(out=aSP_t[:, par, :, c0:TG], in_=aE_t[:, par, :, c0:TG], func=AF.Ln, bias=1.0),
                      baE[par], baSP[par])

            def S3(stp):
                kb, K, c0, c0_prev, par = stp["kb"], stp["K"], stp["c0"], stp["c0_prev"], stp["par"]
                if kb > 0:
                    def gc():
                        ins = None
                        for x in range(2):
                            ins = nc.tensor.matmul(PS[bc][:, c0:TG], lhsT=(NONA, NONB)[x][:, :], rhs=aSP[par][x][:, c0:TG],
                                                   start=(kb == K - 1 and x == 0), stop=(x == 1), skip_group_check=True)
                        return ins
                    tr.op(pe, gc, baSP[par] + [bCONST], [bPS[bc]])
                    S4d(stp)
                def g2():
                    ins = None
                    for x in range(2):
                        bzx = stp["bz"][x]
                        ins = nc.tensor.matmul(PS[bzx][:, c0:TG], lhsT=NTRI[:, :], rhs=aSP[par][x][:, c0:TG], start=False,
                                               stop=(c0_prev is None), skip_group_check=True)
                        if c0_prev is not None:
                            ins = nc.tensor.matmul(PS[bzx][:, c0_prev:TG], lhsT=(SELA, SELB)[x][:, :], rhs=aRR[1 - par][:, c0_prev:TG],
                                                   start=False, stop=True, skip_group_check=True)
                    return ins
                rd = baSP[par] + [bCONST] + ([baRR[1 - par]] if c0_prev is not None else [])
                tr.op(pe, g2, rd, [bPS[stp["bz"][0]], bPS[stp["bz"][1]]])

            def S4d(stp):
                c0, par = stp["c0"], stp["par"]
                if stp["kb"] > 0:
                    tr.op(dve, lambda: nc.vector.tensor_copy(out=aHT[par][:, c0:TG], in_=PS[bc][:, c0:TG]), [bPS[bc]], [baHT[par]])
                    tr.op(dve, lambda: nc.vector.scalar_tensor_tensor(
                        out=aRR[par][:, c0:TG], in0=aHT[par][:, c0:TG], scalar=NEGM, in1=PS[bc][:, c0:TG],
                        op0=ALU.mult, op1=ALU.add), [baHT[par], bPS[bc], bCOLS], [baRR[par]])

            def S4a(stp):
                c0, par = stp["c0"], stp["par"]
                b0 = stp["bz"][0]
                tr.op(act, lambda: nc.scalar.activation(out=aA_t[:, par, :, c0:TG], in_=PSALL[:, b0:b0 + 2, c0:TG], func=AF.Exp),
                      [bPS[b0], bPS[b0 + 1]], baA[par])

            def S5(stp):
                hp, qg, kb, K, c0, par = stp["hp"], stp["qg"], stp["kb"], stp["K"], stp["c0"], stp["par"]
                def g3():
                    ins = None
                    for x in range(2):
                        ins = nc.tensor.matmul(PS[bo][:, c0:TG], lhsT=VZ[:, kb, x, :], rhs=aA[par][x][:, c0:TG],
                                               start=(kb == K - 1 and x == 0), stop=(x == 1), skip_group_check=True)
                    return ins
                tr.op(pe, g3, [bVZ] + baA[par], [bPS[bo]])
                if stp["last"]:
                    tr.op(act, lambda: nc.scalar.activation(out=aOSQ[:, :], in_=PS[bo][:, :], func=AF.Square), [bPS[bo]], [bOSQ])
                    bs = stp["bz"][0]
                    tr.op(pe, lambda: nc.tensor.matmul(PS[bs][:, :], lhsT=BD[:, :], rhs=aOSQ[:, :], start=True, stop=True),
                          [bCONST, bOSQ], [bPS[bs]])
                    i_sd = t2()
                    tr.op(act, lambda: nc.scalar.activation(out=T2[i_sd][:, :], in_=PS[bs][:, :], func=AF.Ln, bias=EPS, scale=1.0 / HD),
                          [bPS[bs]], [bT2[i_sd]])
                    tr.op(act, lambda: nc.scalar.activation(out=T2[i_sd][:, :], in_=T2[i_sd][:, :], func=AF.Exp, scale=-0.5),
                          [bT2[i_sd]], [bT2[i_sd]])
                    i_t = t2()
                    tr.op(dve, lambda: nc.vector.scalar_tensor_tensor(
                        out=T2[i_t][:, :], in0=PS[bo][:, :], scalar=col(cb + 16), in1=T2[i_sd][:, :], op0=ALU.mult, op1=ALU.mult),
                        [bPS[bo], bT2[i_sd], bCOLS], [bT2[i_t]])
                    tr.op(dve, lambda: nc.vector.tensor_tensor(
                        out=SG[:, hp, qg * TG:(qg + 1) * TG], in0=T2[i_t][:, :], in1=SG[:, hp, qg * TG:(qg + 1) * TG], op=ALU.mult),
                        [bT2[i_t], bSG[hp][qg]], [bSG[hp][qg]])

            ns = len(steps)
            preloaded = set()

            def early_load(tg):
                load_h(hsrc, tg, [bHS[tg]] if l > 0 else [], tg % 2)
                preloaded.add(tg)

            SV(0)
            S1(steps[0])
            for i in range(ns + 1):
                if i + 1 < ns:
                    S1(steps[i + 1])
                    if steps[i + 1]["first"] and steps[i + 1]["hp"] == 3 and steps[i + 1]["qg"] == 0:
                        early_load(0)
                if i < ns:
                    S2(steps[i])
                    S3(steps[i])
                if i >= 1:
                    S4a(steps[i - 1])
                    S5(steps[i - 1])
                    if steps[i - 1]["last"] and steps[i - 1]["qg"] == 0 and steps[i - 1]["hp"] < 3:
                        SV(steps[i - 1]["hp"] + 1)
                        if steps[i - 1]["hp"] == 2:
                            early_load(1)

            last = (l == NL - 1)
            tail_r = {}

            def load_pt(tg):
                tr.op(pool, lambda: nc.gpsimd.dma_start(
                    out=PT2[tg % 2][:, :, :], in_=pT[l, :, tg * TG:(tg + 1) * TG].rearrange("(c p) n -> p c n", p=P)),
                    [], [bPT2[tg % 2]], chan=ch_p)

            HOOKS = (0, 2, 3, 4)

            def run_hosted(hosted, m):
                if m in HOOKS:
                    for g in hosted:
                        try:
                            next(g)
                        except StopIteration:
                            pass

            def flush(hosted):
                for g in hosted:
                    for _ in g:
                        pass

            def tail_A(tg, hosted=()):
                hs = tg % 2
                HBx, bHBx = HBs[hs], bHBs[hs]
                if tg not in preloaded:
                    load_h(hsrc, tg, [bHS[tg]] if l > 0 else [], hs)
                for m in range(8):
                    run_hosted(hosted, m)
                    b = fm_matmul(WOUT, bWOUT, m, tg, rhs_fn=lambda k, tg=tg: SG[:, k, tg * TG:(tg + 1) * TG],
                                  rhs_bufs=[bSG[k][tg] for k in range(8)])
                    tr.op(dve, lambda b=b, m=m: nc.vector.tensor_tensor(out=HBx[:, m, :], in0=PS[b][:, :], in1=HBx[:, m, :], op=ALU.add),
                          [bPS[b]] + bHBx[m], bHBx[m])
                flush(hosted)

            def norm_gen(kind, tg):
                hs = tg % 2
                HBx, bHBx, SQx, bSQx = HBs[hs], bHBs[hs], SQs[hs], bSQs[hs]
                allb = lambda c0_: [u for c in range(c0_, c0_ + 4) for u in bHBx[c]]
                if kind == "B":
                    gcol0, out_fn = cb + 8, (lambda c: (HN2s[hs][:, c, :], bHN2s[hs][c]))
                elif not last:
                    gcol0, out_fn = (l + 1) * LC, (lambda c: (XR[:, c, tg * TG:(tg + 1) * TG], [bXR[c][tg]]))
                    dv = hscr[:, tg * TG:(tg + 1) * TG].rearrange("(c p) n -> p c n", p=P)
                    for c0_ in (0, 4):
                        tr.op(sp, lambda c0_=c0_: nc.sync.dma_start(out=dv[:, c0_:c0_ + 4, :], in_=HBx[:, c0_:c0_ + 4, :]),
                              allb(c0_), [bHS[tg]], chan=ch_h)
                else:
                    gcol0, out_fn = NL * LC, (lambda c: (HBx[:, c, :], bHBx[c]))
                for c in range(8):
                    tr.op(act, lambda c=c: nc.scalar.activation(out=SQx[:, c, :], in_=HBx[:, c, :], func=AF.Square), bHBx[c], bSQx[c])
                yield
                i_r = norm_stats(HBx, bHBx, SQx, bSQx, have_sq=True)
                yield
                for half in range(2):
                    for c in range(4 * half, 4 * half + 4):
                        oap, obufs = out_fn(c)
                        tr.op(dve, lambda c=c, oap=oap: nc.vector.scalar_tensor_tensor(
                            out=oap, in0=HBx[:, c, :], scalar=col(gcol0 + c), in1=T2[i_r][:, :], op0=ALU.mult, op1=ALU.mult),
                            bHBx[c] + [bT2[i_r], bCOLS], obufs)
                    if half == 0:
                        yield
                if kind == "D" and last:
                    dv = outT[:, tg * TG:(tg + 1) * TG].rearrange("(c p) n -> p c n", p=P)
                    for c0_ in (0, 4):
                        tr.op(sp, lambda c0_=c0_: nc.sync.dma_start(out=dv[:, c0_:c0_ + 4, :], in_=HBx[:, c0_:c0_ + 4, :]),
                              allb(c0_), [bOUT], chan=ch_o)

            def tail_C(tg, hosted=()):
                hs = tg % 2
                HBx, bHBx, HN2x, bHN2x = HBs[hs], bHBs[hs], HN2s[hs], bHN2s[hs]
                PT, bPT = PT2[tg % 2], bPT2[tg % 2]
                if tg == 0:
                    load_pt(0)
                if tg + 1 < NTG:
                    load_pt(tg + 1)
                for m in range(8):
                    run_hosted(hosted, m)
                    b1 = fm_matmul(WGATE, bWGATE, m, tg, rhs_fn=lambda k: HN2x[:, k, :], rhs_bufs=[u for k in range(8) for u in bHN2x[k]])
                    b2 = fm_matmul(WPLE, bWPLE, m, tg, nk=2, rhs_fn=lambda k: PT[:, k, :], rhs_bufs=[bPT])
                    i = t2()
                    tr.op(act, lambda b1=b1, i=i: nc.scalar.activation(out=T2[i][:, :], in_=PS[b1][:, :], func=AF.Sigmoid),
                          [bPS[b1]], [bT2[i]])
                    tr.op(dve, lambda b2=b2, i=i: nc.vector.tensor_tensor(out=T2[i][:, :], in0=PS[b2][:, :], in1=T2[i][:, :], op=ALU.mult),
                          [bPS[b2], bT2[i]], [bT2[i]])
                    tr.op(dve, lambda i=i, m=m: nc.vector.tensor_tensor(out=HBx[:, m, :], in0=HBx[:, m, :], in1=T2[i][:, :], op=ALU.add),
                          bHBx[m] + [bT2[i]], bHBx[m])
                flush(hosted)

            tail_A(0)
            gB0 = norm_gen("B", 0)
            next(gB0)
            tail_A(1, [gB0])
            tail_C(0, [norm_gen("B", 1)])
            tail_C(1, [norm_gen("D", 0)])
            tail_A(2, [norm_gen("D", 1)])
            gB2 = norm_gen("B", 2)
            next(gB2)
            tail_A(3, [gB2])
            tail_C(2, [norm_gen("B", 3)])
            tail_C(3, [norm_gen("D", 2)])
            flush([norm_gen("D", 3)])

        tr.wait_all(sp, ring_hw.chans + ring_sw.chans)
        tr.wait_all(sp, [pe.chan, act.chan, dve.chan, pool.chan])
    return nc


_NC = None


def _consts():
    c = np.zeros((P, C_END), np.float32)
    p = np.arange(P)[:, None]
    j = np.arange(P)[None, :]
    c[:, C_ID:C_ID + P] = (p == j)
    c[:, C_NEG:C_NEG + P] = np.where(p >= j, -30000.0, 0.0)
    c[:, C_NTRI:C_NTRI + P] = np.where(p >= j, -1.0, 0.0)
    c[:, C_SELA:C_SELA + P] = ((p == 0) | (p == 32)) * np.ones((1, P))
    c[:, C_SELB:C_SELB + P] = ((p == 64) | (p == 96)) * np.ones((1, P))
    c[:, C_BD:C_BD + P] = ((p // 64) == (j // 64))
    c[:, C_NONA:C_NONA + 64] = -1.0
    c[:, C_NONB + 64:C_NONB + P] = -1.0
    return c


def _cols(norm_g, ple_norm_g, attn_out_g, dw_w, dw_b, conv_ln_g, conv_ln_b, conv_out_g, final_g):
    c = np.zeros((P, NCOL), np.float32)
    for l in range(NL):
        b = l * LC
        c[:, b:b + 8] = norm_g[l].reshape(8, P).T
        c[:, b + 8:b + 16] = ple_norm_g[l].reshape(8, P).T
        c[:, b + 16] = np.tile(attn_out_g[l], 2)
        c[:, b + 17:b + 21] = dw_b[l].reshape(4, P).T
        c[:, b + 21:b + 25] = conv_ln_g[l].reshape(4, P).T
        c[:, b + 25:b + 29] = conv_ln_b[l].reshape(4, P).T
        c[:, b + 29:b + 33] = conv_out_g[l].reshape(4, P).T
        c[:, b + 33:b + 33 + 4 * CW] = dw_w[l].reshape(CW, 4, P).transpose(2, 1, 0).reshape(P, 4 * CW)
    c[:, NL * LC:NL * LC + 8] = final_g.reshape(8, P).T
    c[:, NL * LC + 8] = np.where((np.arange(P) // 32) % 2 == 1, -1.0, 0.0)
    return c


def kernel(x, p, norm_g, w_in, attn_out_g, dw_w, dw_b, conv_ln_g, conv_ln_b, w_pw, conv_out_g, w_out,
           ple_norm_g, w_ple_gate, w_ple, final_g):
    global _NC
    f = lambda a: np.ascontiguousarray(np.asarray(a, dtype=np.float32))
    x = f(x)
    p = f(p)
    B = x.shape[0]
    cols = _cols(f(norm_g), f(ple_norm_g), f(attn_out_g), f(dw_w), f(dw_b), f(conv_ln_g), f(conv_ln_b), f(conv_out_g), f(final_g))
    cst = _consts()
    shared = {"w_in": f(w_in), "w_pw": f(w_pw), "w_out": f(w_out), "w_gate": f(w_ple_gate), "w_ple": f(w_ple),
              "cols": cols, "cst": cst}
    in_maps = []
    for b in range(B):
        m = dict(shared)
        m["xT"] = np.ascontiguousarray(x[b].T)
        m["pT"] = np.ascontiguousarray(p[:, b].transpose(0, 2, 1))
        in_maps.append(m)
    if _NC is None:
        _NC = build()
    res = run_bass_kernel_spmd(_NC, in_maps, core_ids=list(range(B)))
    out = np.stack([np.ascontiguousarray(r["outT"].T) for r in res.results], axis=0)
    return out.astype(np.float32)
```

```python
import contextlib
import numpy as np
import concourse.bass as bass
import concourse.mybir as mybir
from concourse.bass_utils import run_bass_kernel_spmd

F32 = mybir.dt.float32
BF16 = mybir.dt.bfloat16
F16 = mybir.dt.float16
AF = mybir.ActivationFunctionType
ALU = mybir.AluOpType

P = 128
S = 2048
D = 1024
NL = 2
DIN = 3584
NH = 8
HD = 64
CW = 31
PLE = 256
TG = 512
NTG = S // TG
EPS = 1e-6
LC = 157
NCOL = NL * LC + 9
C_ID, C_NEG, C_NTRI, C_SELA, C_SELB, C_BD, C_NONA, C_NONB, C_END = 0, 128, 256, 384, 512, 640, 768, 896, 1024


class Buf:
    __slots__ = ("name", "w", "r")

    def __init__(self, name):
        self.name = name
        self.w = None
        self.r = {}


class Chan:
    def __init__(self, sem, step):
        self.sem = sem
        self.count = 0
        self.step = step


class Ring:
    def __init__(self, chans):
        self.chans = chans
        self.i = 0

    def next(self):
        c = self.chans[self.i % len(self.chans)]
        self.i += 1
        return c


class Eng:
    def __init__(self, raw, chan, safe):
        self.raw = raw
        self.chan = chan
        self.known = {}
        self.safe = safe


class T:
    def __init__(self, nc, stack):
        self.nc = nc
        self.stack = stack
        self.nsem = 0
        mk = self.chan
        self.pe = Eng(nc.tensor, mk(1), True)
        self.act = Eng(nc.scalar, mk(1), False)
        self.dve = Eng(nc.vector, mk(1), False)
        self.pool = Eng(nc.gpsimd, mk(1), False)
        self.sp = Eng(nc.sync, mk(1), False)

    def chan(self, step=16):
        self.nsem += 1
        sem = self.stack.enter_context(self.nc.semaphore("sm%d" % self.nsem))
        return Chan(sem, step)

    def op(self, eng, fn, reads=(), writes=(), chan=None):
        deps = {}

        def add(d):
            if d is None:
                return
            c, v = d
            if deps.get(c, 0) < v:
                deps[c] = v

        for b in reads:
            add(b.w)
        for b in writes:
            add(b.w)
            for c, v in b.r.items():
                add((c, v))
        if isinstance(chan, Ring):
            chan = chan.next()
            if chan.count > 0:
                deps[chan] = max(deps.get(chan, 0), chan.count)
        target = chan or eng.chan
        for c, v in deps.items():
            if c is eng.chan and eng.safe and chan is None:
                continue
            if eng.known.get(c, 0) >= v:
                continue
            eng.raw.wait_ge(c.sem, v)
            eng.known[c] = v
        ins = fn()
        target.count += target.step
        ins.then_inc(target.sem, target.step)
        for b in writes:
            b.w = (target, target.count)
            b.r = {}
        for b in reads:
            if b.r.get(target, 0) < target.count:
                b.r[target] = target.count
        return ins

    def wait_all(self, eng, chans):
        for c in chans:
            if c.count > 0:
                eng.raw.wait_ge(c.sem, c.count)


def build():
    nc = bass.Bass("TRN2", target_bir_lowering=False)
    dt = nc.dram_tensor
    xT = dt("xT", [D, S], F32, kind="ExternalInput").ap()
    pT = dt("pT", [NL, PLE, S], F32, kind="ExternalInput").ap()
    w_in = dt("w_in", [NL, D, DIN], F32, kind="ExternalInput").ap()
    w_pw = dt("w_pw", [NL, 512, 512], F32, kind="ExternalInput").ap()
    w_out = dt("w_out", [NL, D, D], F32, kind="ExternalInput").ap()
    w_gate = dt("w_gate", [NL, D, D], F32, kind="ExternalInput").ap()
    w_ple = dt("w_ple", [NL, PLE, D], F32, kind="ExternalInput").ap()
    cols_d = dt("cols", [P, NCOL], F32, kind="ExternalInput").ap()
    cst_d = dt("cst", [P, C_END], F32, kind="ExternalInput").ap()
    outT = dt("outT", [D, S], F32, kind="ExternalOutput").ap()
    hscr = dt("hscr", [D, S], F32, kind="Internal").ap()
    dgs = dt("dgs", [4, P, CW * P], BF16, kind="Internal").ap()

    with contextlib.ExitStack() as st:
        tr = T(nc, st)
        pe, act, dve, pool, sp = tr.pe, tr.act, tr.dve, tr.pool, tr.sp

        def sb(name, shape, dtype):
            return st.enter_context(nc.sbuf_tensor(name, shape, dtype))

        XR = sb("XR", [P, 8, S], BF16)
        SG = sb("SG", [P, 8, S], BF16)
        QT = sb("QT", [P, 4, S], BF16)
        KT = sb("KT", [P, 4, S], BF16)
        VV = sb("VV", [P, 16, 512], BF16)
        CP = sb("CP", [P, 4, S + 32], BF16)
        WT = sb("WT", [P, 18432], BF16)
        PT2 = [sb("PTt%d" % i, [P, 2, TG], BF16) for i in range(2)]
        T2 = [sb("T2_%d" % i, [P, TG], F32) for i in range(3)]
        AW = sb("AW", [P, 8192], BF16)
        aE_t = AW[:, 0:4096].bitcast(F32).rearrange("p (a b t) -> p a b t", a=2, b=2)
        aE = [[aE_t[:, i, j, :] for j in range(2)] for i in range(2)]
        aSP_t = AW[:, 4096:6144].bitcast(F16).rearrange("p (a b t) -> p a b t", a=2, b=2)
        aSP = [[aSP_t[:, i, j, :] for j in range(2)] for i in range(2)]
        aA_t = AW[:, 6144:8192].rearrange("p (a b t) -> p a b t", a=2, b=2)
        DGB = [AW[:, 0:CW * P].rearrange("p (w j) -> p w j", w=CW), AW[:, 4096:4096 + CW * P].rearrange("p (w j) -> p w j", w=CW)]
        DGBF = [AW[:, 0:CW * P], AW[:, 4096:4096 + CW * P]]
        aA = [[aA_t[:, i, j, :] for j in range(2)] for i in range(2)]
        aHT = [sb("aHT%d" % i, [P, TG], BF16) for i in range(2)]
        aRR = [sb("aRR%d" % i, [P, TG], BF16) for i in range(2)]
        aOSQ = sb("aOSQ", [P, TG], BF16)
        COLS = sb("COLS", [P, NCOL], F32)
        IDB = sb("IDB", [P, P], BF16)
        NEGB = sb("NEGB", [P, P], BF16)
        NTRI = sb("NTRI", [P, P], F16)
        SELA = sb("SELA", [P, P], BF16)
        SELB = sb("SELB", [P, P], BF16)
        BD = sb("BD", [P, P], BF16)
        NONA = sb("NONA", [P, P], F16)
        NONB = sb("NONB", [P, P], F16)
        VZ = sb("VZ", [P, 16, 2, P], BF16)
        ONESB = sb("ONESB", [P, P], BF16)
        NEGM = COLS[:, NL * LC + 8:NL * LC + 9]
        PSALL = st.enter_context(nc.psum_tensor("psall", [P, 8, TG], F32))
        PS = [PSALL[:, i, :] for i in range(8)]

        bXR = [[Buf("xr%d_%d" % (c, t)) for t in range(NTG)] for c in range(8)]
        bSG = [[Buf("sg%d_%d" % (c, t)) for t in range(NTG)] for c in range(8)]
        bQ = [[Buf("q%d_%d" % (c, t)) for t in range(NTG)] for c in range(4)]
        bK = [[Buf("k%d_%d" % (c, t)) for t in range(NTG)] for c in range(4)]
        bV = [Buf("v%d" % i) for i in range(16)]
        bCP = [[Buf("cp%d_%d" % (c, t)) for t in range(NTG)] for c in range(4)]
        bWT = [Buf("wt%d" % i) for i in range(4)]
        bPT2 = [Buf("pt0"), Buf("pt1")]
        bDGB = [[Buf("dgb") for w in range(CW)] for _ in range(2)]
        bDGS = [Buf("dgs%d" % i) for i in range(4)]
        bT2 = [Buf("t2_%d" % i) for i in range(3)]
        baE = [[Buf("e") for j in range(2)] for i in range(2)]
        baSP = [[Buf("sp") for j in range(2)] for i in range(2)]
        baA = [[Buf("a") for j in range(2)] for i in range(2)]
        baHT = [Buf("ht") for i in range(2)]
        baRR = [Buf("rr") for i in range(2)]
        bOSQ = Buf("osq")
        bCONST = Buf("const")
        bCOLS = Buf("cols")
        bPS = [Buf("ps%d" % i) for i in range(8)]
        bHS = [Buf("hscr%d" % t) for t in range(NTG)]
        bOUT = Buf("out")

        XRF = XR[:, :, :].rearrange("p c s -> p (c s)")
        QTF = QT[:, :, :].rearrange("p c s -> p (c s)")
        KTF = KT[:, :, :].rearrange("p c s -> p (c s)")
        VVF = VV[:, :, :].rearrange("p c s -> p (c s)")
        CPF = CP[:, :, :].rearrange("p c s -> p (c s)")
        QB = QTF.bitcast(F32).rearrange("p (c t) -> p c t", c=8)
        bQB = [[bQ[c // 2][(c % 2) * 2], bQ[c // 2][(c % 2) * 2 + 1]] for c in range(8)]
        KB = KTF.bitcast(F32).rearrange("p (c t) -> p c t", c=8)
        bKB = [[bK[c // 2][(c % 2) * 2], bK[c // 2][(c % 2) * 2 + 1]] for c in range(8)]
        VB = VVF.bitcast(F32).rearrange("p (c t) -> p c t", c=8)
        bVB = [[bV[2 * c], bV[2 * c + 1]] for c in range(8)]

        def cp_units(a0, a1):
            out = []
            for cc in range(4):
                lo, hi = max(a0, cc * (S + 32)), min(a1, (cc + 1) * (S + 32))
                if lo < hi:
                    for t in range(min(3, (lo - cc * (S + 32)) // 512), min(3, (hi - 1 - cc * (S + 32)) // 512) + 1):
                        if bCP[cc][t] not in out:
                            out.append(bCP[cc][t])
            return out

        HBs = [QB, VB]
        bHBs = [bQB, bVB]
        SQs = [KTF[:, 0:4096].rearrange("p (c t) -> p c t", c=8), CPF[:, 0:4096].rearrange("p (c t) -> p c t", c=8)]
        bSQs = [[[bK[c // 4][c % 4]] for c in range(8)], [cp_units(c * 512, (c + 1) * 512) for c in range(8)]]
        HN2s = [KTF[:, 4096:8192].rearrange("p (c t) -> p c t", c=8), CPF[:, 4096:8192].rearrange("p (c t) -> p c t", c=8)]
        bHN2s = [[[bK[2 + c // 4][c % 4]] for c in range(8)], [cp_units(4096 + c * 512, 4096 + (c + 1) * 512) for c in range(8)]]
        ACCs = [QB[:, 0:4, :], QB[:, 4:8, :]]
        bACCs = [bQB[0:4], bQB[4:8]]
        C2s = [KB[:, 0:4, :], KB[:, 4:8, :]]
        bC2s = [bKB[0:4], bKB[4:8]]
        ACBs = [VV[:, 0:4, :], VV[:, 4:8, :]]
        bACBs = [[[bV[i]] for i in range(0, 4)], [[bV[i]] for i in range(4, 8)]]
        SQCs = [VV[:, 8:12, :], VV[:, 12:16, :]]
        bSQCs = [[[bV[i]] for i in range(8, 12)], [[bV[i]] for i in range(12, 16)]]

        def xr_units(u0, n):
            return [bXR[u // 4][u % 4] for u in range(u0, u0 + n)]

        WPW = WT[:, 12288:14336].rearrange("p (c n) -> p c n", c=4)
        bWPW = bWT[3]
        QZ = [[XRF[:, (28 + 2 * a + x) * 512:(29 + 2 * a + x) * 512] for x in range(2)] for a in range(2)]
        bQZ = [[xr_units(28 + 2 * a + x, 1)[0] for x in range(2)] for a in range(2)]
        bVZ = Buf("vz")
        ring_sw = Ring([tr.chan() for _ in range(6)])
        ring_hw = Ring([tr.chan() for _ in range(6)])
        ch_w = ch_c = ch_p = ring_sw
        ch_h = ch_o = ring_hw

        def col(i):
            return COLS[:, i:i + 1]

        tr.op(sp, lambda: nc.sync.dma_start(out=COLS[:, :], in_=cols_d[:, :]), [], [bCOLS], chan=ch_h)
        bONES = Buf("ones")
        tr.op(dve, lambda: nc.vector.memset(ONESB[:, :], 1.0), [], [bONES])

        def load_consts():
            for dst, c0, c1 in ((IDB, C_ID, C_NEG), (NEGB, C_NEG, C_NTRI), (NTRI, C_NTRI, C_SELA), (SELA, C_SELA, C_SELB),
                                (SELB, C_SELB, C_BD), (BD, C_BD, C_NONA), (NONA, C_NONA, C_NONB), (NONB, C_NONB, C_END)):
                tr.op(pool, lambda dst=dst, c0=c0, c1=c1: nc.gpsimd.dma_start(out=dst[:, :], in_=cst_d[:, c0:c1]),
                      [], [bCONST], chan=ch_c)
            tr.op(pool, lambda: nc.gpsimd.memset(VZ[:, :, :, :], 0.0), [], [bVZ])

        gen_rr = [0]

        def gbank():
            i = gen_rr[0] % 8
            gen_rr[0] += 1
            return i

        t2_rr = [0]

        def t2():
            i = t2_rr[0] % 3
            t2_rr[0] += 1
            return i

        def norm_stats(hb_ap, hb_bufs, sq_ap, sq_bufs, nchunk=8, inv_n=1.0 / D, have_sq=False):
            for c in range(0 if have_sq else nchunk):
                tr.op(act, lambda c=c: nc.scalar.activation(out=sq_ap[:, c, :], in_=hb_ap[:, c, :], func=AF.Square),
                      hb_bufs[c], sq_bufs[c])
            b = gbank()

            def mm():
                ins = None
                for c in range(nchunk):
                    ins = nc.tensor.matmul(PS[b][:, :], lhsT=ONESB[:, :], rhs=sq_ap[:, c, :], start=(c == 0), stop=(c == nchunk - 1))
                return ins
            tr.op(pe, mm, [bONES] + [u for c in range(nchunk) for u in sq_bufs[c]], [bPS[b]])
            i_r = t2()
            tr.op(act, lambda: nc.scalar.activation(out=T2[i_r][:, :], in_=PS[b][:, :], func=AF.Ln, bias=EPS, scale=inv_n),
                  [bPS[b]], [bT2[i_r]])
            tr.op(act, lambda: nc.scalar.activation(out=T2[i_r][:, :], in_=T2[i_r][:, :], func=AF.Exp, scale=-0.5),
                  [bT2[i_r]], [bT2[i_r]])
            return i_r

        def norm_apply(hb_ap, hb_bufs, gcol0, i_r, out_fn, nchunk=8):
            for c in range(nchunk):
                oap, obufs = out_fn(c)
                tr.op(dve, lambda c=c, oap=oap: nc.vector.scalar_tensor_tensor(
                    out=oap, in0=hb_ap[:, c, :], scalar=col(gcol0 + c), in1=T2[i_r][:, :], op0=ALU.mult, op1=ALU.mult),
                    hb_bufs[c] + [bT2[i_r], bCOLS], obufs)

        def load_h(src, tg, src_bufs, hs):
            sv = src[:, tg * TG:(tg + 1) * TG].rearrange("(c p) n -> p c n", p=P)
            for c0_ in (0, 4):
                tr.op(sp, lambda c0_=c0_: nc.sync.dma_start(out=HBs[hs][:, c0_:c0_ + 4, :], in_=sv[:, c0_:c0_ + 4, :]),
                      src_bufs, [u for c in range(c0_, c0_ + 4) for u in bHBs[hs][c]], chan=ch_h)

        N1SQ, bN1SQ = [SQs[0], HN2s[0]], [bSQs[0], bHN2s[0]]
        load_h(xT, 0, [], 0)
        load_h(xT, 1, [], 1)

        def n1_squares(tg):
            hs = tg % 2
            for c in range(8):
                tr.op(act, lambda c=c: nc.scalar.activation(out=N1SQ[hs][:, c, :], in_=HBs[hs][:, c, :], func=AF.Square),
                      bHBs[hs][c], bN1SQ[hs][c])

        def n1_rest(tg):
            hs = tg % 2
            i_r = norm_stats(HBs[hs], bHBs[hs], N1SQ[hs], bN1SQ[hs], have_sq=True)
            norm_apply(HBs[hs], bHBs[hs], 0, i_r, lambda c, tg=tg: (XR[:, c, tg * TG:(tg + 1) * TG], [bXR[c][tg]]))
            if tg + 2 < NTG:
                load_h(xT, tg + 2, [], hs)

        n1_squares(0)
        n1_rest(0)

        for l in range(NL):
            cb = l * LC
            hsrc = xT if l == 0 else hscr
            wslot = [0]

            def load_w(dram_ap, nk, ncols, slots, col0=0):
                n = nk * ncols
                base = slots[0] * 4096 + col0
                dst = WT[:, base:base + n].rearrange("p (c n) -> p c n", c=nk)
                src = dram_ap.rearrange("(c p) n -> p c n", p=P)
                step = max(1, 2048 // ncols)
                for k0 in range(0, nk, step):
                    k1 = min(nk, k0 + step)
                    tr.op(pool, lambda k0=k0, k1=k1: nc.gpsimd.dma_start(out=dst[:, k0:k1, :], in_=src[:, k0:k1, :]),
                          [], [bWT[s_] for s_ in slots], chan=ch_w)
                return dst

            def win_group(j):
                s_ = wslot[0] % 3
                wslot[0] += 1
                return load_w(w_in[l, :, j * 512:(j + 1) * 512], 8, 512, [s_]), bWT[s_]

            def fm_matmul(w_ap, wbuf, m, tg, nk=8, rhs_fn=None, rhs_bufs=None):
                b = gbank()
                if rhs_fn is None:
                    rhs_fn = lambda k: XR[:, k, tg * TG:(tg + 1) * TG]
                    rhs_bufs = [bXR[k][tg] for k in range(nk)]

                def mm():
                    ins = None
                    for k in range(nk):
                        ins = nc.tensor.matmul(PS[b][:, :], lhsT=w_ap[:, k, m * P:(m + 1) * P], rhs=rhs_fn(k),
                                               start=(k == 0), stop=(k == nk - 1))
                    return ins
                tr.op(pe, mm, wbuf + rhs_bufs, [bPS[b]])
                return b

            wv, bv = win_group(4)
            wg, bg = win_group(5)
            wq, bq = win_group(6)
            for c in range(4):
                tr.op(pool, lambda c=c: nc.gpsimd.memset(CP[:, c, 0:32], 0.0), [], [bCP[c][0]])
            if l == 0:
                load_consts()
            for tg in range(NTG):
                if l == 0 and tg + 1 < NTG:
                    n1_squares(tg + 1)
                for m in range(4):
                    if l == 0 and tg + 1 < NTG and m == 2:
                        n1_rest(tg + 1)
                    b1 = fm_matmul(wv, [bv], m, tg)
                    b2 = fm_matmul(wg, [bg], m, tg)
                    i = t2()
                    tr.op(act, lambda b2=b2, i=i: nc.scalar.activation(out=T2[i][:, :], in_=PS[b2][:, :], func=AF.Sigmoid),
                          [bPS[b2]], [bT2[i]])
                    tr.op(dve, lambda b1=b1, i=i, m=m, tg=tg: nc.vector.tensor_tensor(
                        out=CP[:, m, 30 + tg * TG:30 + (tg + 1) * TG], in0=PS[b1][:, :], in1=T2[i][:, :], op=ALU.mult),
                        [bPS[b1], bT2[i]], [bCP[m][tg]] + ([bCP[m][tg + 1]] if tg + 1 < NTG else []))
            for tg in range(NTG):
                for m in range(4):
                    b1 = fm_matmul(wq, [bq], m, tg)
                    tr.op(act, lambda b1=b1, m=m, tg=tg: nc.scalar.activation(
                        out=SG[:, 4 + m, tg * TG:(tg + 1) * TG], in_=PS[b1][:, :], func=AF.Silu), [bPS[b1]], [bSG[4 + m][tg]])
            tr.op(pool, lambda: nc.gpsimd.dma_start(out=WPW[:, :, :], in_=w_pw[l].rearrange("(c p) n -> p c n", p=P)),
                  [], [bWPW], chan=ch_w)
            wq0 = win_group(0)
            wq1 = win_group(1)

            dg_rr = [0]

            prepped = set()

            def conv_prep(tg, ch):
                if (tg, ch) in prepped or tg >= NTG:
                    return
                prepped.add((tg, ch))
                rb_ = (tg * 4 + ch) % 2
                if tg == 0:
                    for w in range(CW):
                        if w % 3 == 2:
                            tr.op(act, lambda w=w: nc.scalar.activation(
                                out=DGB[rb_][:, w, :], in_=IDB[:, :], func=AF.Copy, scale=col(cb + 33 + ch * CW + w)),
                                [bCONST, bCOLS], [bDGB[rb_][w]])
                        else:
                            tr.op(dve, lambda w=w: nc.vector.tensor_scalar(
                                out=DGB[rb_][:, w, :], in0=IDB[:, :], scalar1=col(cb + 33 + ch * CW + w), scalar2=None, op0=ALU.mult),
                                [bCONST, bCOLS], [bDGB[rb_][w]])
                    tr.op(sp, lambda: nc.sync.dma_start(out=dgs[ch, :, :], in_=DGBF[rb_]), bDGB[rb_], [bDGS[ch]], chan=ch_h)
                else:
                    tr.op(sp, lambda: nc.sync.dma_start(out=DGBF[rb_], in_=dgs[ch, :, :]), [bDGS[ch]], bDGB[rb_], chan=ch_h)

            def conv_taps(tg, cs_):
                ACC, bACC, ACB, bACB, SQC, bSQC = ACCs[cs_], bACCs[cs_], ACBs[cs_], bACBs[cs_], SQCs[cs_], bSQCs[cs_]
                conv_prep(tg, 0)
                for ch in range(4):
                    if ch > 0:
                        yield
                    if ch < 3:
                        conv_prep(tg, ch + 1)
                    else:
                        conv_prep(tg + 1, 0)
                    rb_ = (tg * 4 + ch) % 2
                    b = gbank()
                    rb = [bCP[ch][tg]] + ([bCP[ch][tg + 1]] if tg + 1 < NTG else [])
                    for w in range(CW):
                        if w == 15:
                            yield
                        tr.op(pe, lambda w=w: nc.tensor.matmul(
                            PS[b][:, :], lhsT=DGB[rb_][:, w, :], rhs=CP[:, ch, tg * TG + w:tg * TG + w + TG],
                            start=(w == 0), stop=(w == CW - 1)), [bDGB[rb_][w]] + rb, [bPS[b]])
                    bia = col(cb + 17 + ch)
                    tr.op(act, lambda: nc.scalar.activation(
                        out=ACC[:, ch, :], in_=PS[b][:, :], func=AF.Identity, bias=bia), [bPS[b], bCOLS], bACC[ch])
                    tr.op(pool, lambda: nc.gpsimd.tensor_copy(out=ACB[:, ch, :], in_=ACC[:, ch, :]), bACC[ch], bACB[ch])
                    tr.op(act, lambda: nc.scalar.activation(out=SQC[:, ch, :], in_=ACC[:, ch, :], func=AF.Square), bACC[ch], bSQC[ch])

            def conv_post(tg, cs_):
                ACC, bACC, ACB, bACB, SQC, bSQC = ACCs[cs_], bACCs[cs_], ACBs[cs_], bACBs[cs_], SQCs[cs_], bSQCs[cs_]
                C2, bC2 = C2s[cs_], bC2s[cs_]
                CS, bCS = ACB, bACB
                bm = gbank()

                def mm_mean():
                    ins = None
                    for ch in range(4):
                        ins = nc.tensor.matmul(PS[bm][:, :], lhsT=ONESB[:, :], rhs=ACB[:, ch, :], start=(ch == 0), stop=(ch == 3))
                    return ins
                tr.op(pe, mm_mean, [bCONST] + [u for ch in range(4) for u in bACB[ch]], [bPS[bm]])
                bq2 = gbank()

                def mm_sq():
                    ins = None
                    for ch in range(4):
                        ins = nc.tensor.matmul(PS[bq2][:, :], lhsT=ONESB[:, :], rhs=SQC[:, ch, :], start=(ch == 0), stop=(ch == 3))
                    return ins
                tr.op(pe, mm_sq, [bCONST] + [u for ch in range(4) for u in bSQC[ch]], [bPS[bq2]])
                i_mean = t2()
                tr.op(act, lambda: nc.scalar.mul(out=T2[i_mean][:, :], in_=PS[bm][:, :], mul=1.0 / 512), [bPS[bm]], [bT2[i_mean]])
                i_var = t2()
                tr.op(dve, lambda: nc.vector.tensor_tensor(out=T2[i_var][:, :], in0=T2[i_mean][:, :], in1=T2[i_mean][:, :], op=ALU.mult),
                      [bT2[i_mean]], [bT2[i_var]])
                tr.op(dve, lambda: nc.vector.scalar_tensor_tensor(
                    out=T2[i_var][:, :], in0=PS[bq2][:, :], scalar=1.0 / 512, in1=T2[i_var][:, :], op0=ALU.mult, op1=ALU.subtract),
                    [bPS[bq2], bT2[i_var]], [bT2[i_var]])
                tr.op(act, lambda: nc.scalar.activation(out=T2[i_var][:, :], in_=T2[i_var][:, :], func=AF.Ln, bias=EPS, scale=1.0),
                      [bT2[i_var]], [bT2[i_var]])
                tr.op(act, lambda: nc.scalar.activation(out=T2[i_var][:, :], in_=T2[i_var][:, :], func=AF.Exp, scale=-0.5),
                      [bT2[i_var]], [bT2[i_var]])
                for ch in range(4):
                    tr.op(pool, lambda ch=ch: nc.gpsimd.tensor_tensor(out=ACC[:, ch, :], in0=ACC[:, ch, :], in1=T2[i_mean][:, :], op=ALU.subtract),
                          bACC[ch] + [bT2[i_mean]], bACC[ch])
                    tr.op(dve, lambda ch=ch: nc.vector.tensor_tensor(out=ACC[:, ch, :], in0=ACC[:, ch, :], in1=T2[i_var][:, :], op=ALU.mult),
                          bACC[ch] + [bT2[i_var]], bACC[ch])
                    tr.op(act, lambda ch=ch: nc.scalar.activation(
                        out=CS[:, ch, :], in_=ACC[:, ch, :], func=AF.Silu, bias=col(cb + 25 + ch), scale=col(cb + 21 + ch)),
                        bACC[ch] + [bCOLS], bCS[ch])
                yield
                for m in range(4):
                    b = fm_matmul(WPW, [bWPW], m, tg, nk=4, rhs_fn=lambda k: CS[:, k, :],
                                  rhs_bufs=[u for k in range(4) for u in bCS[k]])
                    tr.op(act, lambda b=b, m=m: nc.scalar.copy(out=C2[:, m, :], in_=PS[b][:, :]), [bPS[b]], bC2[m])
                    tr.op(act, lambda b=b, m=m: nc.scalar.activation(out=SQC[:, m, :], in_=PS[b][:, :], func=AF.Square), [bPS[b]], bSQC[m])
                yield
                i_r = norm_stats(C2, bC2, SQC, bSQC, nchunk=4, inv_n=1.0 / 512, have_sq=True)
                norm_apply(C2, bC2, cb + 29, i_r, lambda m: (C2[:, m, :], bC2[m]), nchunk=4)
                for m in range(4):
                    tr.op(dve, lambda m=m: nc.vector.tensor_tensor(
                        out=SG[:, 4 + m, tg * TG:(tg + 1) * TG], in0=C2[:, m, :], in1=SG[:, 4 + m, tg * TG:(tg + 1) * TG], op=ALU.mult),
                        bC2[m] + [bSG[4 + m][tg]], [bSG[4 + m][tg]])

            def wq_chunk(m):
                wq, bq = wq0
                for tg in range(NTG):
                    b1 = fm_matmul(wq, [bq], m, tg)
                    tr.op(dve, lambda b1=b1, tg=tg: nc.vector.tensor_scalar(
                        out=QT[:, m, tg * TG:(tg + 1) * TG], in0=PS[b1][:, :], scalar1=HD ** -0.5, scalar2=None, op0=ALU.mult),
                        [bPS[b1]], [bQ[m][tg]])

            def wk_chunk(m):
                wq, bq = wq1
                for tg in range(NTG):
                    b1 = fm_matmul(wq, [bq], m, tg)
                    tr.op(dve, lambda b1=b1, tg=tg: nc.vector.tensor_copy(
                        out=KT[:, m, tg * TG:(tg + 1) * TG], in_=PS[b1][:, :]), [bPS[b1]], [bK[m][tg]])

            def last_fill():
                wq_chunk(0)
                yield
                wq_chunk(1)
                yield
                wk_chunk(0)
                yield
                wk_chunk(1)

            def interleave(ga, gb, points):
                seg = 0
                la, lb = True, True
                while la:
                    try:
                        next(ga)
                    except StopIteration:
                        la = False
                    if seg in points and lb:
                        try:
                            next(gb)
                        except StopIteration:
                            lb = False
                    seg += 1
                while lb:
                    try:
                        next(gb)
                    except StopIteration:
                        lb = False

            for _ in conv_taps(0, 0):
                pass
            for tg in range(NTG):
                if tg + 1 < NTG:
                    interleave(conv_taps(tg + 1, (tg + 1) % 2), conv_post(tg, tg % 2), (0, 3, 5))
                else:
                    interleave(last_fill(), conv_post(tg, tg % 2), (0, 1, 2))

            wq_chunk(2)
            wq_chunk(3)
            wk_chunk(2)
            wk_chunk(3)
            wq, bq = win_group(2)
            wg3 = win_group(3)
            for tb in range(16):
                b = gbank()

                def mm(b=b, tb=tb, wq=wq):
                    ins = None
                    for k in range(8):
                        ins = nc.tensor.matmul(PS[b][:, :], lhsT=XR[:, k, tb * P:(tb + 1) * P], rhs=wq[:, k, :],
                                               start=(k == 0), stop=(k == 7))
                    return ins
                tr.op(pe, mm, [bq] + [bXR[k][tb // 4] for k in range(8)], [bPS[b]])
                tr.op(act, lambda b=b, tb=tb: nc.scalar.copy(out=VV[:, tb, :], in_=PS[b][:, :]), [bPS[b]], [bV[tb]])
            wq, bq = wg3
            for m in range(4):
                for tg in range(NTG):
                    b1 = fm_matmul(wq, [bq], m, tg)
                    tr.op(act, lambda b1=b1, m=m, tg=tg: nc.scalar.activation(
                        out=SG[:, m, tg * TG:(tg + 1) * TG], in_=PS[b1][:, :], func=AF.Silu), [bPS[b1]], [bSG[m][tg]])

            WOUT = load_w(w_out[l], 8, 1024, [0, 1])
            bWOUT = [bWT[0], bWT[1]]
            WGATE = load_w(w_gate[l], 8, 1024, [2, 3])
            bWGATE = [bWT[2], bWT[3]]
            WPLE = load_w(w_ple[l], 2, 1024, [3], col0=4096)
            bWPLE = [bWT[3]]

            steps = []
            for hp in range(4):
                for qg in (3, 2, 1, 0):
                    K = 4 * qg + 4
                    for kb in range(K - 1, -1, -1):
                        steps.append(dict(hp=hp, qg=qg, kb=kb, K=K, c0=(0 if kb < 4 * qg else P * (kb - 4 * qg)),
                                          diag=(kb >= 4 * qg), first=(kb == K - 1), last=(kb == 0)))
            sidx = -1
            for i, stp in enumerate(steps):
                if stp["first"]:
                    sidx += 1
                stp["sp"] = sidx % 2
                stp["c0_prev"] = None if stp["first"] else steps[i - 1]["c0"]
                stp["bz"] = [(i % 3) * 2, (i % 3) * 2 + 1]
                stp["par"] = i % 2
            bo, bc = 6, 7
            prs = [(0, 64), (64, 128)]

            for a in range(2):
                tr.op(dve, lambda a=a: nc.vector.memset(QZ[a][0][64:128, :], 0.0), [], [bQZ[a][0]])
                tr.op(dve, lambda a=a: nc.vector.memset(QZ[a][1][0:64, :], 0.0), [], [bQZ[a][1]])

            def S0(stp):
                hp, qg, sp_ = stp["hp"], stp["qg"], stp["sp"]
                q0 = qg * TG
                for x in range(2):
                    p0, p1 = prs[x]
                    tr.op(dve, lambda x=x, p0=p0, p1=p1: nc.vector.tensor_copy(out=QZ[sp_][x][p0:p1, :], in_=QT[p0:p1, hp, q0:q0 + TG]),
                          [bQ[hp][qg]], [bQZ[sp_][x]])

            def SV(hp):
                for x in range(2):
                    h = 2 * hp + x
                    tr.op(dve, lambda x=x, h=h: nc.vector.tensor_copy(out=VZ[:, :, x, x * HD:(x + 1) * HD], in_=VV[:, :, h * HD:(h + 1) * HD]),
                          bV, [bVZ])

            def S1(stp):
                hp, qg, kb, c0, diag, sp_ = stp["hp"], stp["qg"], stp["kb"], stp["c0"], stp["diag"], stp["sp"]
                if stp["first"]:
                    S0(stp)
                q0 = qg * TG
                def g1():
                    ins = None
                    for x in range(2):
                        bzx = stp["bz"][x]
                        ins = nc.tensor.matmul(PS[bzx][:, c0:TG], lhsT=KT[:, hp, kb * P:(kb + 1) * P],
                                               rhs=QZ[sp_][x][:, c0:TG], start=True, stop=True)
                        if diag:
                            ins = nc.tensor.matmul(PS[bzx][:, c0:c0 + P], lhsT=IDB[:, :], rhs=NEGB[:, :], start=False, stop=True,
                                                   skip_group_check=True)
                    return ins
                tr.op(pe, g1, [bK[hp][kb // 4], bQZ[sp_][0], bQZ[sp_][1], bCONST], [bPS[stp["bz"][0]], bPS[stp["bz"][1]]])

            def S2(stp):
                c0, par = stp["c0"], stp["par"]
                b0 = stp["bz"][0]
                tr.op(act, lambda: nc.scalar.activation(out=aE_t[:, par, :, c0:TG], in_=PSALL[:, b0:b0 + 2, c0:TG], func=AF.Exp),
                      [bPS[b0], bPS[b0 + 1]], baE[par])
                tr.op(act, lambda: nc.scalar.activation(out=aSP_t[:, par, :, c0:TG], in_=aE_t[:, par, :, c0:TG], func=AF.Ln, bias=1.0),
                      baE[par], baSP[par])

            def S3(stp):
                kb, K, c0, c0_prev, par = stp["kb"], stp["K"], stp["c0"], stp["c0_prev"], stp["par"]
                if kb > 0:
                    def gc():
                        ins = None
                        for x in range(2):
                            ins = nc.tensor.matmul(PS[bc][:, c0:TG], lhsT=(NONA, NONB)[x][:, :], rhs=aSP[par][x][:, c0:TG],
                                                   start=(kb == K - 1 and x == 0), stop=(x == 1), skip_group_check=True)
                        return ins
                    tr.op(pe, gc, baSP[par] + [bCONST], [bPS[bc]])
                    S4d(stp)
                def g2():
                    ins = None
                    for x in range(2):
                        bzx = stp["bz"][x]
                        ins = nc.tensor.matmul(PS[bzx][:, c0:TG], lhsT=NTRI[:, :], rhs=aSP[par][x][:, c0:TG], start=False,
                                               stop=(c0_prev is None), skip_group_check=True)
                        if c0_prev is not None:
                            ins = nc.tensor.matmul(PS[bzx][:, c0_prev:TG], lhsT=(SELA, SELB)[x][:, :], rhs=aRR[1 - par][:, c0_prev:TG],
                                                   start=False, stop=True, skip_group_check=True)
                    return ins
                rd = baSP[par] + [bCONST] + ([baRR[1 - par]] if c0_prev is not None else [])
                tr.op(pe, g2, rd, [bPS[stp["bz"][0]], bPS[stp["bz"][1]]])

            def S4d(stp):
                c0, par = stp["c0"], stp["par"]
                if stp["kb"] > 0:
                    tr.op(dve, lambda: nc.vector.tensor_copy(out=aHT[par][:, c0:TG], in_=PS[bc][:, c0:TG]), [bPS[bc]], [baHT[par]])
                    tr.op(dve, lambda: nc.vector.scalar_tensor_tensor(
                        out=aRR[par][:, c0:TG], in0=aHT[par][:, c0:TG], scalar=NEGM, in1=PS[bc][:, c0:TG],
                        op0=ALU.mult, op1=ALU.add), [baHT[par], bPS[bc], bCOLS], [baRR[par]])

            def S4a(stp):
                c0, par = stp["c0"], stp["par"]
                b0 = stp["bz"][0]
                tr.op(act, lambda: nc.scalar.activation(out=aA_t[:, par, :, c0:TG], in_=PSALL[:, b0:b0 + 2, c0:TG], func=AF.Exp),
                      [bPS[b0], bPS[b0 + 1]], baA[par])

            def S5(stp):
                hp, qg, kb, K, c0, par = stp["hp"], stp["qg"], stp["kb"], stp["K"], stp["c0"], stp["par"]
                def g3():
                    ins = None
                    for x in range(2):
                        ins = nc.tensor.matmul(PS[bo][:, c0:TG], lhsT=VZ[:, kb, x, :], rhs=aA[par][x][:, c0:TG],
                                               start=(kb == K - 1 and x == 0), stop=(x == 1), skip_group_check=True)
                    return ins
                tr.op(pe, g3, [bVZ] + baA[par], [bPS[bo]])
                if stp["last"]:
                    tr.op(act, lambda: nc.scalar.activation(out=aOSQ[:, :], in_=PS[bo][:, :], func=AF.Square), [bPS[bo]], [bOSQ])
                    bs = stp["bz"][0]
                    tr.op(pe, lambda: nc.tensor.matmul(PS[bs][:, :], lhsT=BD[:, :], rhs=aOSQ[:, :], start=True, stop=True),
                          [bCONST, bOSQ], [bPS[bs]])
                    i_sd = t2()
                    tr.op(act, lambda: nc.scalar.activation(out=T2[i_sd][:, :], in_=PS[bs][:, :], func=AF.Ln, bias=EPS, scale=1.0 / HD),
                          [bPS[bs]], [bT2[i_sd]])
                    tr.op(act, lambda: nc.scalar.activation(out=T2[i_sd][:, :], in_=T2[i_sd][:, :], func=AF.Exp, scale=-0.5),
                          [bT2[i_sd]], [bT2[i_sd]])
                    i_t = t2()
                    tr.op(dve, lambda: nc.vector.scalar_tensor_tensor(
                        out=T2[i_t][:, :], in0=PS[bo][:, :], scalar=col(cb + 16), in1=T2[i_sd][:, :], op0=ALU.mult, op1=ALU.mult),
                        [bPS[bo], bT2[i_sd], bCOLS], [bT2[i_t]])
                    tr.op(dve, lambda: nc.vector.tensor_tensor(
                        out=SG[:, hp, qg * TG:(qg + 1) * TG], in0=T2[i_t][:, :], in1=SG[:, hp, qg * TG:(qg + 1) * TG], op=ALU.mult),
                        [bT2[i_t], bSG[hp][qg]], [bSG[hp][qg]])

            ns = len(steps)
            SV(0)
            S1(steps[0])
            for i in range(ns + 1):
                if i + 1 < ns:
                    S1(steps[i + 1])
                if i < ns:
                    S2(steps[i])
                    S3(steps[i])
                if i >= 1:
                    S4a(steps[i - 1])
                    S5(steps[i - 1])
                    if steps[i - 1]["last"] and steps[i - 1]["qg"] == 0 and steps[i - 1]["hp"] < 3:
                        SV(steps[i - 1]["hp"] + 1)

            last = (l == NL - 1)
            tail_r = {}

            def load_pt(tg):
                tr.op(pool, lambda: nc.gpsimd.dma_start(
                    out=PT2[tg % 2][:, :, :], in_=pT[l, :, tg * TG:(tg + 1) * TG].rearrange("(c p) n -> p c n", p=P)),
                    [], [bPT2[tg % 2]], chan=ch_p)

            HOOKS = (0, 2, 3, 4)

            def run_hosted(hosted, m):
                if m in HOOKS:
                    for g in hosted:
                        try:
                            next(g)
                        except StopIteration:
                            pass

            def flush(hosted):
                for g in hosted:
                    for _ in g:
                        pass

            def tail_A(tg, hosted=()):
                hs = tg % 2
                HBx, bHBx = HBs[hs], bHBs[hs]
                load_h(hsrc, tg, [bHS[tg]] if l > 0 else [], hs)
                for m in range(8):
                    run_hosted(hosted, m)
                    b = fm_matmul(WOUT, bWOUT, m, tg, rhs_fn=lambda k, tg=tg: SG[:, k, tg * TG:(tg + 1) * TG],
                                  rhs_bufs=[bSG[k][tg] for k in range(8)])
                    tr.op(dve, lambda b=b, m=m: nc.vector.tensor_tensor(out=HBx[:, m, :], in0=PS[b][:, :], in1=HBx[:, m, :], op=ALU.add),
                          [bPS[b]] + bHBx[m], bHBx[m])
                flush(hosted)

            def norm_gen(kind, tg):
                hs = tg % 2
                HBx, bHBx, SQx, bSQx = HBs[hs], bHBs[hs], SQs[hs], bSQs[hs]
                allb = lambda c0_: [u for c in range(c0_, c0_ + 4) for u in bHBx[c]]
                if kind == "B":
                    gcol0, out_fn = cb + 8, (lambda c: (HN2s[hs][:, c, :], bHN2s[hs][c]))
                elif not last:
                    gcol0, out_fn = (l + 1) * LC, (lambda c: (XR[:, c, tg * TG:(tg + 1) * TG], [bXR[c][tg]]))
                    dv = hscr[:, tg * TG:(tg + 1) * TG].rearrange("(c p) n -> p c n", p=P)
                    for c0_ in (0, 4):
                        tr.op(sp, lambda c0_=c0_: nc.sync.dma_start(out=dv[:, c0_:c0_ + 4, :], in_=HBx[:, c0_:c0_ + 4, :]),
                              allb(c0_), [bHS[tg]], chan=ch_h)
                else:
                    gcol0, out_fn = NL * LC, (lambda c: (HBx[:, c, :], bHBx[c]))
                for c in range(8):
                    tr.op(act, lambda c=c: nc.scalar.activation(out=SQx[:, c, :], in_=HBx[:, c, :], func=AF.Square), bHBx[c], bSQx[c])
                yield
                i_r = norm_stats(HBx, bHBx, SQx, bSQx, have_sq=True)
                yield
                for half in range(2):
                    for c in range(4 * half, 4 * half + 4):
                        oap, obufs = out_fn(c)
                        tr.op(dve, lambda c=c, oap=oap: nc.vector.scalar_tensor_tensor(
                            out=oap, in0=HBx[:, c, :], scalar=col(gcol0 + c), in1=T2[i_r][:, :], op0=ALU.mult, op1=ALU.mult),
                            bHBx[c] + [bT2[i_r], bCOLS], obufs)
                    if half == 0:
                        yield
                if kind == "D" and last:
                    dv = outT[:, tg * TG:(tg + 1) * TG].rearrange("(c p) n -> p c n", p=P)
                    for c0_ in (0, 4):
                        tr.op(sp, lambda c0_=c0_: nc.sync.dma_start(out=dv[:, c0_:c0_ + 4, :], in_=HBx[:, c0_:c0_ + 4, :]),
                              allb(c0_), [bOUT], chan=ch_o)

            def tail_C(tg, hosted=()):
                hs = tg % 2
                HBx, bHBx, HN2x, bHN2x = HBs[hs], bHBs[hs], HN2s[hs], bHN2s[hs]
                PT, bPT = PT2[tg % 2], bPT2[tg % 2]
                if tg == 0:
                    load_pt(0)
                if tg + 1 < NTG:
                    load_pt(tg + 1)
                for m in range(8):
                    run_hosted(hosted, m)
                    b1 = fm_matmul(WGATE, bWGATE, m, tg, rhs_fn=lambda k: HN2x[:, k, :], rhs_bufs=[u for k in range(8) for u in bHN2x[k]])
                    b2 = fm_matmul(WPLE, bWPLE, m, tg, nk=2, rhs_fn=lambda k: PT[:, k, :], rhs_bufs=[bPT])
                    i = t2()
                    tr.op(act, lambda b1=b1, i=i: nc.scalar.activation(out=T2[i][:, :], in_=PS[b1][:, :], func=AF.Sigmoid),
                          [bPS[b1]], [bT2[i]])
                    tr.op(dve, lambda b2=b2, i=i: nc.vector.tensor_tensor(out=T2[i][:, :], in0=PS[b2][:, :], in1=T2[i][:, :], op=ALU.mult),
                          [bPS[b2], bT2[i]], [bT2[i]])
                    tr.op(dve, lambda i=i, m=m: nc.vector.tensor_tensor(out=HBx[:, m, :], in0=HBx[:, m, :], in1=T2[i][:, :], op=ALU.add),
                          bHBx[m] + [bT2[i]], bHBx[m])
                flush(hosted)

            tail_A(0)
            gB0 = norm_gen("B", 0)
            next(gB0)
            tail_A(1, [gB0])
            tail_C(0, [norm_gen("B", 1)])
            tail_C(1, [norm_gen("D", 0)])
            tail_A(2, [norm_gen("D", 1)])
            gB2 = norm_gen("B", 2)
            next(gB2)
            tail_A(3, [gB2])
            tail_C(2, [norm_gen("B", 3)])
            tail_C(3, [norm_gen("D", 2)])
            flush([norm_gen("D", 3)])

        tr.wait_all(sp, ring_hw.chans + ring_sw.chans)
        tr.wait_all(sp, [pe.chan, act.chan, dve.chan, pool.chan])
    return nc


_NC = None


def _consts():
    c = np.zeros((P, C_END), np.float32)
    p = np.arange(P)[:, None]
    j = np.arange(P)[None, :]
    c[:, C_ID:C_ID + P] = (p == j)
    c[:, C_NEG:C_NEG + P] = np.where(p >= j, -30000.0, 0.0)
    c[:, C_NTRI:C_NTRI + P] = np.where(p >= j, -1.0, 0.0)
    c[:, C_SELA:C_SELA + P] = ((p == 0) | (p == 32)) * np.ones((1, P))
    c[:, C_SELB:C_SELB + P] = ((p == 64) | (p == 96)) * np.ones((1, P))
    c[:, C_BD:C_BD + P] = ((p // 64) == (j // 64))
    c[:, C_NONA:C_NONA + 64] = -1.0
    c[:, C_NONB + 64:C_NONB + P] = -1.0
    return c


def _cols(norm_g, ple_norm_g, attn_out_g, dw_w, dw_b, conv_ln_g, conv_ln_b, conv_out_g, final_g):
    c = np.zeros((P, NCOL), np.float32)
    for l in range(NL):
        b = l * LC
        c[:, b:b + 8] = norm_g[l].reshape(8, P).T
        c[:, b + 8:b + 16] = ple_norm_g[l].reshape(8, P).T
        c[:, b + 16] = np.tile(attn_out_g[l], 2)
        c[:, b + 17:b + 21] = dw_b[l].reshape(4, P).T
        c[:, b + 21:b + 25] = conv_ln_g[l].reshape(4, P).T
        c[:, b + 25:b + 29] = conv_ln_b[l].reshape(4, P).T
        c[:, b + 29:b + 33] = conv_out_g[l].reshape(4, P).T
        c[:, b + 33:b + 33 + 4 * CW] = dw_w[l].reshape(CW, 4, P).transpose(2, 1, 0).reshape(P, 4 * CW)
    c[:, NL * LC:NL * LC + 8] = final_g.reshape(8, P).T
    c[:, NL * LC + 8] = np.where((np.arange(P) // 32) % 2 == 1, -1.0, 0.0)
    return c


def kernel(x, p, norm_g, w_in, attn_out_g, dw_w, dw_b, conv_ln_g, conv_ln_b, w_pw, conv_out_g, w_out,
           ple_norm_g, w_ple_gate, w_ple, final_g):
    global _NC
    f = lambda a: np.ascontiguousarray(np.asarray(a, dtype=np.float32))
    x = f(x)
    p = f(p)
    B = x.shape[0]
    cols = _cols(f(norm_g), f(ple_norm_g), f(attn_out_g), f(dw_w), f(dw_b), f(conv_ln_g), f(conv_ln_b), f(conv_out_g), f(final_g))
    cst = _consts()
    shared = {"w_in": f(w_in), "w_pw": f(w_pw), "w_out": f(w_out), "w_gate": f(w_ple_gate), "w_ple": f(w_ple),
              "cols": cols, "cst": cst}
    in_maps = []
    for b in range(B):
        m = dict(shared)
        m["xT"] = np.ascontiguousarray(x[b].T)
        m["pT"] = np.ascontiguousarray(p[:, b].transpose(0, 2, 1))
        in_maps.append(m)
    if _NC is None:
        _NC = build()
    res = run_bass_kernel_spmd(_NC, in_maps, core_ids=list(range(B)))
    out = np.stack([np.ascontiguousarray(r["outT"].T) for r in res.results], axis=0)
    return out.astype(np.float32)
```
